# Optimizing a Trainium2 kernel written in Bass

```python
import math
import jax, jax.numpy as jnp
from jax import lax
import numpy as np

D_MODEL = 1024
BATCH = 8
SEQ = 2048
DEPTH = 1

D_MIX = D_MODEL
D_HYENA = D_MIX // 2
D_ATTN = D_MIX - D_HYENA
HEAD_DIM = 64
N_HEADS = D_ATTN // HEAD_DIM
HYENA_ORDER = 2
SHORT_CONV = 3
FILTER_EMB = 33
FILTER_BANDS = (FILTER_EMB - 1) // 2
FILTER_WIDTH = 64
N_DIRS = 2
DECAY_TARGET = 1e-2
FAST_DECAY_PCT = 0.3
SLOW_DECAY_PCT = 1.5
DILATED_PATTERNS = ((128, 1), (512, 4), (2048, 16))
D_FF = 2816
RMS_EPS = 1e-6
NEG_INF = -1e30

kernel_name = "hybrid_hyena_dilated_attn_macaron_block"


def rmsnorm(x, g):
    xf = x.astype(jnp.float32)
    y = xf * lax.rsqrt(jnp.mean(xf * xf, axis=-1, keepdims=True) + RMS_EPS)
    return (y * g.astype(jnp.float32)).astype(x.dtype)


def swiglu(h, w_gate, w_up, w_down):
    return (jax.nn.silu(h @ w_gate) * (h @ w_up)) @ w_down


def alibi_slopes():
    return jnp.asarray(np.array([2.0 ** (-8.0 * (i + 1) / N_HEADS) for i in range(N_HEADS)], np.float32))


def short_conv(u, w, b):
    L = u.shape[1]
    r = SHORT_CONV // 2
    up = jnp.pad(u, ((0, 0), (r, r), (0, 0)))
    y = b
    for i in range(SHORT_CONV):
        y = y + up[:, i:i + L] * w[i]
    return y


def hyena_filters(L, fw1, fb1, fw2, fb2, fw3, fb3, fw_out, f_freq):
    t = jnp.linspace(0.0, 1.0, L, dtype=jnp.float32)[:, None]
    w = 2.0 * math.pi * jnp.arange(L, dtype=jnp.float32)[:, None] / L
    f = jnp.linspace(1e-4, FILTER_BANDS - 1, FILTER_BANDS, dtype=jnp.float32)[None, :]
    z = jnp.concatenate([t, jnp.cos(f * w), -jnp.sin(f * w)], axis=-1)
    freq = f_freq.astype(jnp.float32)
    h = jnp.sin(freq * (z @ fw1.astype(jnp.float32) + fb1.astype(jnp.float32)))
    h = jnp.sin(freq * (h @ fw2.astype(jnp.float32) + fb2.astype(jnp.float32)))
    h = jnp.sin(freq * (h @ fw3.astype(jnp.float32) + fb3.astype(jnp.float32)))
    k = (h @ fw_out.astype(jnp.float32)).reshape(L, HYENA_ORDER, N_DIRS, D_HYENA)
    max_decay = math.log(DECAY_TARGET) / FAST_DECAY_PCT
    min_decay = math.log(DECAY_TARGET) / SLOW_DECAY_PCT
    deltas = jnp.linspace(min_decay, max_decay, D_HYENA, dtype=jnp.float32)
    decay = jnp.exp(-t * jnp.abs(deltas)[None, :])
    return k * decay[:, None, None, :]


def bidir_long_conv(u, h_fwd, h_bwd, skip):
    L = u.shape[1]
    h_full = jnp.concatenate([h_fwd, jnp.zeros_like(h_fwd[:1]), h_bwd[:0:-1]], axis=0)
    Hf = jnp.fft.rfft(h_full, axis=0)
    U = jnp.fft.rfft(u, n=2 * L, axis=1)
    y = jnp.fft.irfft(U * Hf[None], n=2 * L, axis=1)[:, :L]
    return y + u * skip


def hyena_mixer(u3, conv_w, conv_b, fw1, fb1, fw2, fb2, fw3, fb3, fw_out, f_freq, f_skip):
    L = u3.shape[1]
    u = short_conv(u3.astype(jnp.float32), conv_w.astype(jnp.float32), conv_b.astype(jnp.float32))
    v, x1, x2 = jnp.split(u, 3, axis=-1)
    filt = hyena_filters(L, fw1, fb1, fw2, fb2, fw3, fb3, fw_out, f_freq)
    skip = f_skip.astype(jnp.float32)
    z = v
    for o, gate in enumerate((x1, x2)):
        z = gate * bidir_long_conv(z, filt[:, o, 0], filt[:, o, 1], skip[o])
    return z


def dilated_window_attention(q, k, v, window, dilation, slopes):
    B, S, H, Dh = q.shape
    R = window // (2 * dilation)
    Lsub = S // dilation
    nblk = -(-Lsub // R)
    Lp = nblk * R

    def to_sub(a):
        return a.reshape(B, Lsub, dilation, H, Dh).transpose(0, 2, 3, 1, 4)

    qs, ks, vs = to_sub(q), to_sub(k), to_sub(v)
    qs = jnp.pad(qs, ((0, 0), (0, 0), (0, 0), (0, Lp - Lsub), (0, 0)))
    pad_kv = ((0, 0), (0, 0), (0, 0), (R, Lp - Lsub + R), (0, 0))
    ks, vs = jnp.pad(ks, pad_kv), jnp.pad(vs, pad_kv)
    qb = qs.reshape(B, dilation, H, nblk, R, Dh)

    def band(a):
        ab = a.reshape(B, dilation, H, nblk + 2, R, Dh)
        return jnp.concatenate([ab[:, :, :, :nblk], ab[:, :, :, 1:nblk + 1], ab[:, :, :, 2:]], axis=4)

    kb, vb = band(ks), band(vs)
    scores = jnp.einsum('bdhnqc,bdhnkc->bdhnqk', qb, kb) / math.sqrt(Dh)
    qi = jnp.arange(nblk)[:, None, None] * R + jnp.arange(R)[None, :, None]
    kj = jnp.arange(nblk)[:, None, None] * R + jnp.arange(3 * R)[None, None, :] - R
    rel = kj - qi
    valid = (jnp.abs(rel) <= R) & (kj >= 0) & (kj < Lsub)
    alibi = -slopes[:, None, None, None] * (dilation * jnp.abs(rel)).astype(jnp.float32)[None]
    scores = jnp.where(valid, scores + alibi[None, None], NEG_INF)
    m = jnp.max(scores, axis=-1, keepdims=True)
    p = jnp.exp(scores - m)
    den = jnp.sum(p, axis=-1, keepdims=True)
    out = jnp.einsum('bdhnqk,bdhnkc->bdhnqc', p, vb) / den
    lse = (m + jnp.log(den))[..., 0]
    out = out.reshape(B, dilation, H, Lp, Dh)[:, :, :, :Lsub].transpose(0, 3, 1, 2, 4).reshape(B, S, H, Dh)
    lse = lse.reshape(B, dilation, H, Lp)[..., :Lsub].transpose(0, 3, 1, 2).reshape(B, S, H)
    return out, lse


def dilated_attention_mixer(a3):
    B, S, _ = a3.shape
    a3 = a3.astype(jnp.float32)
    q, k, v = [t.reshape(B, S, N_HEADS, HEAD_DIM) for t in jnp.split(a3, 3, axis=-1)]
    slopes = alibi_slopes()
    outs, lses = [], []
    for window, dilation in DILATED_PATTERNS:
        o, l = dilated_window_attention(q, k, v, window, dilation, slopes)
        outs.append(o)
        lses.append(l)
    wts = jax.nn.softmax(jnp.stack(lses, axis=0), axis=0)
    out = jnp.sum(wts[..., None] * jnp.stack(outs, axis=0), axis=0)
    return out.reshape(B, S, D_ATTN)


def setup_inputs(seed: int = 0) -> dict:
    key = jax.random.key(seed)
    ks = jax.random.split(key, 32)
    f32 = jnp.float32
    nrm = lambda k, shape, s: jax.random.normal(k, shape, f32) * s
    gain = lambda k, n: 1.0 + 0.02 * jax.random.normal(k, (n,), f32)
    D, F = D_MODEL, D_FF
    return {
        "x": jax.random.normal(ks[0], (BATCH, SEQ, D), f32),
        "ffn1_norm_g": gain(ks[1], D),
        "ffn1_w_gate": nrm(ks[2], (D, F), D ** -0.5),
        "ffn1_w_up": nrm(ks[3], (D, F), D ** -0.5),
        "ffn1_w_down": nrm(ks[4], (F, D), F ** -0.5),
        "mix_norm_g": gain(ks[5], D),
        "w_in": nrm(ks[6], (D, 3 * D_HYENA + 3 * D_ATTN), D ** -0.5),
        "hy_conv_w": nrm(ks[7], (SHORT_CONV, 3 * D_HYENA), SHORT_CONV ** -0.5),
        "hy_conv_b": nrm(ks[8], (3 * D_HYENA,), 0.02),
        "hy_filt_w1": nrm(ks[9], (FILTER_EMB, FILTER_WIDTH), FILTER_EMB ** -0.5),
        "hy_filt_b1": nrm(ks[10], (FILTER_WIDTH,), 0.1),
        "hy_filt_w2": nrm(ks[11], (FILTER_WIDTH, FILTER_WIDTH), FILTER_WIDTH ** -0.5),
        "hy_filt_b2": nrm(ks[12], (FILTER_WIDTH,), 0.1),
        "hy_filt_w3": nrm(ks[13], (FILTER_WIDTH, FILTER_WIDTH), FILTER_WIDTH ** -0.5),
        "hy_filt_b3": nrm(ks[14], (FILTER_WIDTH,), 0.1),
        "hy_filt_w_out": nrm(ks[15], (FILTER_WIDTH, HYENA_ORDER * N_DIRS * D_HYENA), 0.1 * FILTER_WIDTH ** -0.5),
        "hy_filt_freq": 1.0 + 0.1 * jax.random.normal(ks[16], (FILTER_WIDTH,), f32),
        "hy_filt_skip": nrm(ks[17], (HYENA_ORDER, D_HYENA), 0.5),
        "hy_out_norm_g": gain(ks[18], D_HYENA),
        "attn_out_norm_g": gain(ks[19], D_ATTN),
        "w_out": nrm(ks[20], (D_MIX, D), D_MIX ** -0.5),
        "ffn2_norm_g": gain(ks[21], D),
        "ffn2_w_gate": nrm(ks[22], (D, F), D ** -0.5),
        "ffn2_w_up": nrm(ks[23], (D, F), D ** -0.5),
        "ffn2_w_down": nrm(ks[24], (F, D), F ** -0.5),
        "final_norm_g": gain(ks[25], D),
    }


def reference(x, ffn1_norm_g, ffn1_w_gate, ffn1_w_up, ffn1_w_down, mix_norm_g, w_in,
              hy_conv_w, hy_conv_b, hy_filt_w1, hy_filt_b1, hy_filt_w2, hy_filt_b2,
              hy_filt_w3, hy_filt_b3, hy_filt_w_out, hy_filt_freq, hy_filt_skip,
              hy_out_norm_g, attn_out_norm_g, w_out, ffn2_norm_g, ffn2_w_gate, ffn2_w_up,
              ffn2_w_down, final_norm_g):
    for _ in range(DEPTH):
        x = x + 0.5 * swiglu(rmsnorm(x, ffn1_norm_g), ffn1_w_gate, ffn1_w_up, ffn1_w_down)
        h = rmsnorm(x, mix_norm_g)
        proj = h @ w_in
        y_hy = hyena_mixer(proj[..., :3 * D_HYENA], hy_conv_w, hy_conv_b,
                           hy_filt_w1, hy_filt_b1, hy_filt_w2, hy_filt_b2,
                           hy_filt_w3, hy_filt_b3, hy_filt_w_out, hy_filt_freq, hy_filt_skip)
        y_at = dilated_attention_mixer(proj[..., 3 * D_HYENA:])
        y = jnp.concatenate([rmsnorm(y_hy, hy_out_norm_g), rmsnorm(y_at, attn_out_norm_g)], axis=-1)
        x = x + (y.astype(x.dtype) @ w_out)
        x = x + 0.5 * swiglu(rmsnorm(x, ffn2_norm_g), ffn2_w_gate, ffn2_w_up, ffn2_w_down)
    return rmsnorm(x, final_norm_g)
```

```python
from contextlib import ExitStack
import math
import numpy as np
import ml_dtypes
import concourse.bass as bass
import concourse.mybir as mybir
from concourse.bass_utils import run_bass_kernel_spmd

F32 = mybir.dt.float32
BF16 = mybir.dt.bfloat16
AF = mybir.ActivationFunctionType
ALU = mybir.AluOpType
AX = mybir.AxisListType

S = 2048
D = 1024
DFF = 2816
NF = DFF // 128
NK = D // 128
EPS = 1e-6
NCORES = 8


class Prog:
    ENG = ("pe", "act", "dve", "pool", "sp")

    def __init__(self, nc):
        self.nc = nc
        self.es = ExitStack()
        self.sem = {}
        self.cnt = {}
        self.ops = {e: [] for e in self.ENG}
        self.known = {e: {} for e in self.ENG}
        for e in self.ENG:
            self.sem[e] = self.es.enter_context(nc.semaphore("s_" + e))
            self.cnt[e] = 0
        self.slot_sem = {}
        self.slot_cnt = {}
        self.last_w = {}
        self.readers = {}
        self.nwaits = 0

    def sb(self, name, shape, dt):
        return self.es.enter_context(self.nc.sbuf_tensor(name, list(shape), dt))

    def ps(self, name, shape, dt=F32):
        return self.es.enter_context(self.nc.psum_tensor(name, list(shape), dt))

    def _deps(self, reads, writes):
        toks = []
        for k in reads:
            t = self.last_w.get(k)
            if t is not None:
                toks.append(t)
        for k in writes:
            t = self.last_w.get(k)
            if t is not None:
                toks.append(t)
            toks.extend(self.readers.get(k, ()))
        return toks

    def _commit(self, reads, writes, tok):
        for k in reads:
            self.readers.setdefault(k, []).append(tok)
        for k in writes:
            self.last_w[k] = tok
            self.readers[k] = []

    def alias(self, new_keys, old_keys):
        toks = []
        for k in old_keys:
            t = self.last_w.get(k)
            if t is not None:
                toks.append(t)
            toks.extend(self.readers.get(k, ()))
        for k in new_keys:
            self.last_w[k] = None
            self.readers[k] = list(toks)

    def _waits(self, e, toks):
        need = {}
        for (s, v) in toks:
            if s == "pe" and e == "pe":
                continue
            if v > need.get(s, 0):
                need[s] = v
        out = []
        kn = self.known[e]
        for s, v in need.items():
            if kn.get(s, 0) >= v:
                continue
            kn[s] = v
            out.append((s, v))
        self.nwaits += len(out)
        return out

    def op(self, e, fn, reads=(), writes=(), extra=()):
        toks = self._deps(reads, writes) + list(extra)
        waits = self._waits(e, toks)
        self.cnt[e] += 1
        tok = (e, self.cnt[e])
        self.ops[e].append((waits, fn, ("eng", e)))
        self._commit(reads, writes, tok)
        return tok

    def dma(self, q, out, in_, reads=(), writes=(), slot=None, extra=(), **kw):
        if slot is None:
            k0 = writes[0] if writes else reads[0]
            slot = "_".join(str(x) for x in k0)
        if slot not in self.slot_sem:
            self.slot_sem[slot] = self.es.enter_context(self.nc.semaphore("q_" + slot))
            self.slot_cnt[slot] = 0
        toks = self._deps(reads, writes) + list(extra)
        waits = self._waits(q, toks)
        self.slot_cnt[slot] += 16
        tok = ("slot:" + slot, self.slot_cnt[slot])

        def fn(eng, out=out, in_=in_, kw=kw):
            return eng.dma_start(out=out, in_=in_, **kw)
        self.ops[q].append((waits, fn, ("slot", slot)))
        self._commit(reads, writes, tok)
        return tok

    def _semof(self, s):
        if s.startswith("slot:"):
            return self.slot_sem[s[5:]]
        return self.sem[s]

    def emit(self, final_tokens):
        nc = self.nc
        engs = {"pe": "tensor", "act": "scalar", "dve": "vector", "pool": "gpsimd", "sp": "sync"}
        fw = self._waits("sp", list(final_tokens))
        with nc.Block() as block:
            for e in self.ENG:
                def body(eng, e=e):
                    for waits, fn, kind in self.ops[e]:
                        for (s, v) in waits:
                            eng.wait_ge(self._semof(s), v)
                        inst = fn(eng)
                        if kind[0] == "eng":
                            inst.then_inc(self.sem[e], 1)
                        else:
                            inst.then_inc(self.slot_sem[kind[1]], 16)
                    if e == "sp":
                        for (s, v) in fw:
                            eng.wait_ge(self._semof(s), v)
                getattr(block, engs[e])(body)

    def close(self):
        self.es.close()


class Arena:
    def __init__(self, P, name, nbytes):
        self.P = P
        self.t = P.sb(name, [128, nbytes // 2], BF16)
        self.size = nbytes // 2
        self.off = 0
        self.name = name
        self.hist = []

    def take(self, nelem, dt, keys):
        w = nelem * (2 if dt == F32 else 1)
        n2 = (w + 15) // 16 * 16
        assert self.off + n2 <= self.size, f"arena {self.name} overflow need {(self.off + n2) * 2} have {self.size * 2}"
        lo, hi = self.off, self.off + n2
        v = self.t[:, lo:lo + w]
        self.off = hi
        old = []
        for (a, b, ks) in self.hist:
            if a < hi and lo < b:
                old.extend(ks)
        keys = list(keys)
        for k in keys:
            assert k not in self.P.last_w and k not in self.P.readers, f"key reuse {k}"
        self.P.alias(keys, old)
        self.hist.append((lo, hi, keys))
        if dt == F32:
            v = v.bitcast(F32)
        return v

    def mark(self):
        return self.off

    def reset(self, m=0):
        self.off = m


class Builder:
    def __init__(self, debug=None):
        self.debug = debug or {}
        nc = bass.Bass("TRN2", target_bir_lowering=False)
        self.nc = nc
        P = Prog(nc)
        self.P = P
        self.final_toks = []
        d = self.dram = {}
        self.uid = 0

        def din(name, shape, dt=F32):
            d[name] = nc.dram_tensor(name, list(shape), dt, kind="ExternalInput").ap()

        def dscr(name, shape, dt):
            d[name] = nc.dram_tensor(name, list(shape), dt, kind="Internal").ap()
        din("xT", [D, S])
        din("gains", [128, 4 * NK])
        din("ident", [128, 128])
        for i in (1, 2):
            din(f"wgu{i}", [NF, 128, 2048])
            din(f"wd{i}", [NK, 128, NF * 128])
        din("w_in_t", [12, 128, 2048])
        din("w_out_t", [NK, 128, 1024])
        din("gout", [128, 8])
        din("convw", [128, 36])
        din("convb", [1, 1536])
        din("abias", [128, 8 * 640], BF16)
        din("cs_cb", [16, 128, 4096], BF16)
        din("zT", [33, S])
        din("decay", [16, 128, 1024])
        din("rot", [128, 48])
        din("filt_small", [64, 8])
        din("fw1", [33, 64])
        din("fw23", [64, 128])
        din("fwout", [64, 2048])
        din("skip", [2, 512])
        dscr("hspec", [2, 16, 128, 1024], F32)
        dscr("pt", [24, 128, S], BF16)
        dscr("ynat", [4, 128, S], BF16)
        d["outT"] = nc.dram_tensor("outT", [D, S], F32, kind="ExternalOutput").ap()

        self.X = P.sb("X", [128, NK, S], F32)
        self.cst = Arena(P, "cst", 7 * 1024)
        self.wsl = Arena(P, "wsl", 23 * 1024)
        self.reg = Arena(P, "reg", 113 * 1024)
        c = self.cst
        self.gains = c.take(4 * NK, F32, [("gains",)])
        self.gout = c.take(8, F32, [("gout",)])
        self.eps_ap = c.take(1, F32, [("eps",)])
        self.ones_bf = c.take(128, BF16, [("ones",)])
        self.ident_f = c.take(128, F32, [("ident_f",)])
        self.ident_bf = c.take(128, BF16, [("ident_bf",)])
        self.rot = c.take(48, F32, [("rot",)])
        self.wgu_slot = [self.wsl.take(2048, BF16, [("wgu", i)]).rearrange("p (m k j) -> p m k j", m=2, k=NK) for i in range(3)]
        self.wd_slot = [self.wsl.take(NF * 128, BF16, [("wd", i, 0), ("wd", i, 1)]).rearrange("p (f j) -> p f j", f=NF) for i in range(2)]
        self.bank = [P.ps(f"bank{i}", [128, 512], F32) for i in range(8)]
        self.wgu_i = 0
        self.wd_i = 0
        self.evi = 0

    def u(self, s):
        self.uid += 1
        return f"{s}{self.uid}"

    def dbg(self, name, ap, reads):
        if name not in self.debug.get("dump", ()):
            return
        shape = list(ap.shape)
        o = self.nc.dram_tensor("dbg_" + name, shape, ap.dtype, kind="ExternalOutput").ap()
        self.final_toks.append(self.P.dma("sp", o, ap, reads=reads, slot="dbg_" + name))

    def evac(self, out, in_, reads, writes, scale=None):
        P = self.P
        self.evi += 1
        if self.evi % 2 == 0:
            if scale is None:
                return P.op("act", lambda e: e.activation(out=out, in_=in_, func=AF.Copy), reads=reads, writes=writes)
            return P.op("act", lambda e: e.activation(out=out, in_=in_, func=AF.Copy, scale=scale), reads=reads, writes=writes)
        if scale is None:
            return P.op("dve", lambda e: e.tensor_copy(out=out, in_=in_), reads=reads, writes=writes)
        return P.op("dve", lambda e: e.tensor_scalar(out=out, in0=in_, scalar1=scale, scalar2=None, op0=ALU.mult),
                    reads=reads, writes=writes)

    def load_consts(self):
        P, d = self.P, self.dram
        P.dma("sp", self.gains, d["gains"], writes=[("gains",)])
        P.dma("sp", self.gout, d["gout"], writes=[("gout",)])
        P.dma("sp", self.ident_f, d["ident"], writes=[("ident_f",)])
        P.dma("sp", self.rot, d["rot"], writes=[("rot",)])
        P.op("dve", lambda e: e.memset(self.ones_bf, 1.0), writes=[("ones",)])
        P.op("dve", lambda e: e.memset(self.eps_ap, EPS), writes=[("eps",)])
        P.op("dve", lambda e: e.tensor_copy(out=self.ident_bf, in_=self.ident_f), reads=[("ident_f",)], writes=[("ident_bf",)])
        for kc in range(NK):
            P.dma("sp", self.X[:, kc, :], d["xT"][kc * 128:(kc + 1) * 128, :], writes=[("X", kc, n) for n in range(4)],
                  slot=f"X{kc}")

    def rmsnorm(self, gidx, emit_out, tag):
        P = self.P
        X = self.X
        reg = self.reg
        sq = [reg.take(512, BF16, [(tag + "sq", i)]) for i in range(4)]
        psN, psR = self.bank[6], self.bank[7]
        qi = 0
        for n in range(4):
            cs = slice(n * 512, (n + 1) * 512)
            for kc in range(NK):
                s = sq[qi % 4]
                sk = (tag + "sq", qi % 4)
                qi += 1
                P.op("act", lambda e, s=s, kc=kc, cs=cs: e.activation(out=s, in_=X[:, kc, cs], func=AF.Square),
                     reads=[("X", kc, n)], writes=[sk])
                P.op("pe", lambda e, s=s, kc=kc: e.matmul(psN[:], lhsT=self.ones_bf, rhs=s, start=(kc == 0), stop=(kc == NK - 1)),
                     reads=[sk, ("ones",)], writes=[("bank", 6)])
            P.op("act", lambda e: e.activation(out=psR[:], in_=psN[:], func=AF.Sqrt, bias=self.eps_ap, scale=1.0 / D),
                 reads=[("bank", 6), ("eps",)], writes=[("bank", 7)])
            P.op("dve", lambda e: e.reciprocal(out=psR[:], in_=psR[:]), reads=[("bank", 7)], writes=[("bank", 7)])
            for kc in range(NK):
                def make(o, kc=kc, cs=cs):
                    return lambda e: e.scalar_tensor_tensor(
                        out=o, in0=X[:, kc, cs], scalar=self.gains[:, gidx * NK + kc:gidx * NK + kc + 1], in1=psR[:],
                        op0=ALU.mult, op1=ALU.mult)
                emit_out(kc, n, cs, make)

    def norm_to_hT(self, gidx, tag):
        P = self.P
        hT = self.reg.take(NK * S, BF16, [(tag + "hT", kc, n) for kc in range(NK) for n in range(4)]).rearrange("p (k t) -> p k t", k=NK)

        def emit(kc, n, cs, make):
            P.op("dve", make(hT[:, kc, cs]), reads=[("X", kc, n), ("bank", 7), ("gains",)], writes=[(tag + "hT", kc, n)])
        self.rmsnorm(gidx, emit, tag)
        return hT

    def issue_gu(self, src):
        si = self.wgu_i % 3
        self.wgu_i += 1
        self.P.dma("pool", self.wgu_slot[si].rearrange("p m k j -> p (m k j)"), src, writes=[("wgu", si)], slot=f"wgu{si}")
        return si

    def issue_wd(self, src, n=NF * 128):
        si = self.wd_i % 2
        self.wd_i += 1
        dst = self.wd_slot[si].rearrange("p f j -> p (f j)")
        half = n // 2
        self.P.dma("pool", dst[:, 0:half], src[:, 0:half], writes=[("wd", si, 0)], slot=f"wd{si}a")
        self.P.dma("pool", dst[:, half:n], src[:, half:n], writes=[("wd", si, 1)], slot=f"wd{si}b")
        return si

    def ffn(self, idx, gidx):
        P, d = self.P, self.dram
        X = self.X
        reg = self.reg
        tag = f"f{idx}"
        m0 = reg.mark()
        hT = self.norm_to_hT(gidx, tag)
        AT = reg.take(NF * 1024, BF16, [(tag + "AT", f, n) for f in range(NF) for n in range(2)]).rearrange("p (f t) -> p f t", f=NF)
        sg = [reg.take(512, F32, [(tag + "sg", i)]) for i in range(2)]
        wgu, wd = d[f"wgu{idx}"], d[f"wd{idx}"]
        gu_jobs = [(h, f) for h in range(2) for f in range(NF)]
        wd_jobs = [(h, dc) for h in range(2) for dc in range(NK)]
        gslot = {}
        dslot = {}

        def igu(j):
            if j < len(gu_jobs):
                gslot[j] = self.issue_gu(wgu[gu_jobs[j][1]])

        def iwd(j):
            if j < len(wd_jobs):
                dslot[j] = self.issue_wd(wd[wd_jobs[j][1]])
        igu(0)
        igu(1)
        iwd(0)
        gj = wj = ci = di = 0
        psG = [self.bank[0], self.bank[1]]
        psU = [self.bank[2], self.bank[3]]
        psD = [self.bank[4], self.bank[5]]
        for h in range(2):
            for f in range(NF):
                igu(gj + 2)
                si = gslot[gj]
                gj += 1
                w = self.wgu_slot[si]
                for n2 in range(2):
                    n = h * 2 + n2
                    cs = slice(n * 512, (n + 1) * 512)
                    b = ci % 2
                    ci += 1

                    def mm(e, m, ps, w=w, cs=cs):
                        r = None
                        for kc in range(NK):
                            r = e.matmul(ps[:], lhsT=w[:, m, kc, :], rhs=hT[:, kc, cs], start=(kc == 0), stop=(kc == NK - 1))
                        return r
                    rd = [("wgu", si)] + [(tag + "hT", kc, n) for kc in range(NK)]
                    P.op("pe", lambda e, mm=mm, b=b: mm(e, 0, psG[b]), reads=rd, writes=[("bank", b)])
                    P.op("pe", lambda e, mm=mm, b=b: mm(e, 1, psU[b]), reads=rd, writes=[("bank", 2 + b)])
                    P.op("act", lambda e, b=b: e.activation(out=sg[b], in_=psG[b][:], func=AF.Silu),
                         reads=[("bank", b)], writes=[(tag + "sg", b)])
                    P.op("dve", lambda e, b=b, f=f, n2=n2: e.tensor_tensor(
                        out=AT[:, f, n2 * 512:(n2 + 1) * 512], in0=sg[b], in1=psU[b][:], op=ALU.mult),
                        reads=[(tag + "sg", b), ("bank", 2 + b)], writes=[(tag + "AT", f, n2)])
            for dc in range(NK):
                iwd(wj + 1)
                si = dslot[wj]
                wj += 1
                w = self.wd_slot[si]
                for n2 in range(2):
                    n = h * 2 + n2
                    cs = slice(n * 512, (n + 1) * 512)
                    b = di % 2
                    di += 1

                    def mmd(e, w=w, n2=n2, b=b):
                        r = None
                        for f in range(NF):
                            r = e.matmul(psD[b][:], lhsT=w[:, f, :], rhs=AT[:, f, n2 * 512:(n2 + 1) * 512],
                                         start=(f == 0), stop=(f == NF - 1))
                        return r
                    P.op("pe", mmd, reads=[("wd", si, 0), ("wd", si, 1)] + [(tag + "AT", f, n2) for f in range(NF)],
                         writes=[("bank", 4 + b)])
                    P.op("dve", lambda e, b=b, dc=dc, cs=cs: e.scalar_tensor_tensor(
                        out=X[:, dc, cs], in0=psD[b][:], scalar=0.5, in1=X[:, dc, cs], op0=ALU.mult, op1=ALU.add),
                        reads=[("bank", 4 + b), ("X", dc, n)], writes=[("X", dc, n)])
        reg.reset(m0)

    def final(self, gidx):
        P, d = self.P, self.dram
        reg = self.reg
        m0 = reg.mark()
        ost = [reg.take(512, F32, [("ost", i)]) for i in range(4)]
        oi = [0]

        def emit(kc, n, cs, make):
            i = oi[0] % 4
            oi[0] += 1
            P.op("dve", make(ost[i]), reads=[("X", kc, n), ("bank", 7), ("gains",)], writes=[("ost", i)])
            self.final_toks.append(P.dma("sp", d["outT"][kc * 128:(kc + 1) * 128, cs], ost[i], reads=[("ost", i)], slot=f"out{i}"))
        self.rmsnorm(gidx, emit, "fin")
        reg.reset(m0)

    def filter_phase(self):
        P, d = self.P, self.dram
        reg = self.reg
        m0 = reg.mark()
        TWO_PI = 2.0 * math.pi
        MAGIC = 12582912.0
        AB = reg.take(2 * 2 * 16 * 512, BF16, [("AB", o, ab, J) for o in range(2) for ab in range(2) for J in range(16)]
                      ).rearrange("p (o a j c) -> p o a j c", o=2, a=2, j=16)
        h3 = reg.take(S + 16, BF16, [("h3", n) for n in range(4)] + [("h3pad",)])
        wob = reg.take(2048, BF16, [("wob",)])
        m1 = reg.mark()
        zT = reg.take(S, F32, [("zT",)])
        fw1 = reg.take(128, F32, [("fw1",)])
        fw2 = reg.take(128, F32, [("fw2",)])
        fw3 = reg.take(128, F32, [("fw3",)])
        fsm = reg.take(8, F32, [("fsm",)])
        fsc = reg.take(4, F32, [("fsc",)])
        hA = reg.take(S, F32, [("hA", n) for n in range(4)])
        hB = reg.take(S, F32, [("hB", n) for n in range(4)])
        tb = [reg.take(512, F32, [("ftmp", i)]) for i in range(3)]
        for ap, k in ((zT, ("zT",)), (fw1, ("fw1",)), (fw2, ("fw2",)), (fw3, ("fw3",)), (fsm, ("fsm",)), (wob, ("wob",))):
            P.op("pool", lambda e, ap=ap: e.memset(ap, 0.0), writes=[k])
        P.dma("sp", zT[0:33, :], d["zT"], writes=[("zT",)])
        P.dma("sp", fw1[0:33, 0:64], d["fw1"], writes=[("fw1",)])
        P.dma("sp", fw2[0:64, 0:64], d["fw23"][:, 0:64], writes=[("fw2",)])
        P.dma("sp", fw3[0:64, 0:64], d["fw23"][:, 64:128], writes=[("fw3",)])
        P.dma("sp", fsm[0:64, :], d["filt_small"], writes=[("fsm",)])
        P.dma("pool", wob[0:64, :], d["fwout"], writes=[("wob",)])
        P.op("dve", lambda e: e.memset(h3[:, S:S + 16], 0.0), writes=[("h3pad",)])
        P.op("dve", lambda e: e.tensor_scalar(out=fsc[:, 0:1], in0=fsm[:, 3:4], scalar1=1.0 / TWO_PI, scalar2=None, op0=ALU.mult),
             reads=[("fsm",)], writes=[("fsc",)])
        for l in range(3):
            P.op("dve", lambda e, l=l: e.tensor_tensor(out=fsc[:, 1 + l:2 + l], in0=fsm[:, l:l + 1], in1=fsc[:, 0:1], op=ALU.mult),
                 reads=[("fsm",), ("fsc",)], writes=[("fsc",)])
        layers = [(fw1, ("fw1",), zT, ("zT",), hA, "hA"),
                  (fw2, ("fw2",), hA, "hA", hB, "hB"),
                  (fw3, ("fw3",), hB, "hB", h3, "h3")]
        bi = 0
        for l, (w, wk, src_, srck, dst, dstk) in enumerate(layers):
            for n in range(4):
                cs = slice(n * 512, (n + 1) * 512)
                b = bi % 4
                bi += 1
                ps = self.bank[b]
                rk = [srck] if isinstance(srck, tuple) else [(srck, n)]
                P.op("pe", lambda e, w=w, src_=src_, cs=cs, ps=ps: e.matmul(ps[:], lhsT=w, rhs=src_[:, cs], start=True, stop=True),
                     reads=rk + [wk], writes=[("bank", b)])
                P.op("dve", lambda e, ps=ps, l=l: e.tensor_scalar(out=tb[0], in0=ps[:], scalar1=fsc[:, 0:1],
                                                                scalar2=fsc[:, 1 + l:2 + l], op0=ALU.mult, op1=ALU.add),
                     reads=[("bank", b), ("fsc",)], writes=[("ftmp", 0)])
                P.op("dve", lambda e: e.tensor_scalar(out=tb[1], in0=tb[0], scalar1=MAGIC, scalar2=MAGIC,
                                                      op0=ALU.add, op1=ALU.subtract),
                     reads=[("ftmp", 0)], writes=[("ftmp", 1)])
                P.op("dve", lambda e: e.tensor_tensor(out=tb[2], in0=tb[0], in1=tb[1], op=ALU.subtract),
                     reads=[("ftmp", 0), ("ftmp", 1)], writes=[("ftmp", 2)])
                P.op("act", lambda e, dst=dst, cs=cs: e.activation(out=dst[:, cs], in_=tb[2], func=AF.Sin, scale=6.283185),
                     reads=[("ftmp", 2)], writes=[(dstk, n)])
        self.dbg("h3", h3[0:64, 0:S], [("h3", n) for n in range(4)])
        reg.reset(m1)
        dec = [reg.take(1024, F32, [("dec", i)]).rearrange("p (a c) -> p a c", a=2) for i in range(2)]
        mt = [reg.take(512, F32, [("mt", i)]) for i in range(2)]
        h3k = [("h3", n) for n in range(4)] + [("h3pad",)]
        for J in range(16):
            ds = dec[J % 2]
            P.dma("sp", ds.rearrange("p a c -> p (a c)"), d["decay"][J], writes=[("dec", J % 2)], slot=f"dec{J % 2}")
            for od in range(4):
                off = J * 128 + (od % 2)
                P.op("pe", lambda e, od=od, off=off: e.matmul(self.bank[od][:], lhsT=h3[:, off:off + 128], rhs=wob[:, od * 512:(od + 1) * 512],
                                                              start=True, stop=True),
                     reads=h3k + [("wob",)], writes=[("bank", od)])
            for o in range(2):
                P.op("dve", lambda e, o=o, ds=ds: e.tensor_tensor(out=mt[0], in0=self.bank[2 * o][:], in1=ds[:, 0, :], op=ALU.mult),
                     reads=[("bank", 2 * o), ("dec", J % 2)], writes=[("mt", 0)])
                P.op("dve", lambda e, o=o, ds=ds: e.tensor_tensor(out=mt[1], in0=self.bank[2 * o + 1][:], in1=ds[:, 1, :], op=ALU.mult),
                     reads=[("bank", 2 * o + 1), ("dec", J % 2)], writes=[("mt", 1)])
                P.op("pool", lambda e, o=o, J=J: e.tensor_tensor(out=AB[:, o, 0, J, :], in0=mt[0], in1=mt[1], op=ALU.add),
                     reads=[("mt", 0), ("mt", 1)], writes=[("AB", o, 0, J)])
                P.op("pool", lambda e, o=o, J=J: e.tensor_tensor(out=AB[:, o, 1, J, :], in0=mt[1], in1=mt[0], op=ALU.subtract),
                     reads=[("mt", 0), ("mt", 1)], writes=[("AB", o, 1, J)])
        reg.reset(m1)
        cb = [reg.take(4096, BF16, [("fcb", i)]).rearrange("p (a j c) -> p a j c", a=2, j=16) for i in range(2)]
        hst = [reg.take(1024, F32, [("hst", i)]).rearrange("p (a c) -> p a c", a=2) for i in range(2)]
        skb = reg.take(1024, F32, [("skb",)]).rearrange("p (o c) -> p o c", o=2)
        t12 = [reg.take(512, F32, [("t12", i)]) for i in range(2)]
        P.dma("sp", skb.rearrange("p o c -> p (o c)"), d["skip"].rearrange("(x o) c -> x (o c)", x=1).broadcast_to([128, 1024]),
              writes=[("skb",)])
        P.op("dve", lambda e: e.tensor_scalar(out=skb.rearrange("p o c -> p (o c)"), in0=skb.rearrange("p o c -> p (o c)"),
                                              scalar1=2.0 / 4096.0, scalar2=None, op0=ALU.mult), reads=[("skb",)], writes=[("skb",)])
        rot = self.rot
        hi = 0
        for kt in range(16):
            c_ = cb[kt % 2]
            P.dma("sp", c_.rearrange("p a j c -> p (a j c)"), d["cs_cb"][kt], writes=[("fcb", kt % 2)], slot=f"fcb{kt % 2}")
            for o in range(2):
                ba, bb = 2 * o, 2 * o + 1
                for part, bk in ((0, ba), (1, bb)):
                    def mmf(e, part=part, bk=bk, c_=c_, o=o):
                        r = None
                        for J in range(16):
                            r = e.matmul(self.bank[bk][:], lhsT=c_[:, part, J, :], rhs=AB[:, o, part, J, :], start=(J == 0), stop=(J == 15))
                        return r
                    P.op("pe", mmf, reads=[("fcb", kt % 2)] + [("AB", o, part, J) for J in range(16)], writes=[("bank", bk)])
                hs = hst[hi % 2]
                hk = ("hst", hi % 2)
                hi += 1
                pre, pim = self.bank[ba], self.bank[bb]
                P.op("dve", lambda e, pre=pre, o=o, kt=kt: e.scalar_tensor_tensor(out=t12[0], in0=pre[:], scalar=rot[:, kt:kt + 1], in1=skb[:, o, :],
                                                                                 op0=ALU.mult, op1=ALU.add),
                     reads=[("bank", ba), ("skb",), ("rot",)], writes=[("t12", 0)])
                P.op("dve", lambda e, pim=pim, hs=hs, kt=kt: e.scalar_tensor_tensor(out=hs[:, 0, :], in0=pim[:], scalar=rot[:, 32 + kt:33 + kt], in1=t12[0],
                                                                                   op0=ALU.mult, op1=ALU.add),
                     reads=[("bank", bb), ("t12", 0), ("rot",)], writes=[hk])
                P.op("dve", lambda e, pre=pre, kt=kt: e.tensor_scalar(out=t12[1], in0=pre[:], scalar1=rot[:, 16 + kt:17 + kt], scalar2=None, op0=ALU.mult),
                     reads=[("bank", ba), ("rot",)], writes=[("t12", 1)])
                P.op("dve", lambda e, pim=pim, hs=hs, kt=kt: e.scalar_tensor_tensor(out=hs[:, 1, :], in0=pim[:], scalar=rot[:, kt:kt + 1], in1=t12[1],
                                                                                   op0=ALU.mult, op1=ALU.add),
                     reads=[("bank", bb), ("t12", 1), ("rot",), hk], writes=[hk])
                P.dma("sp", d["hspec"][o, kt], hs.rearrange("p a c -> p (a c)"), reads=[hk], writes=[("hspec", o, kt)], slot=f"hsp{hi % 2}")
        reg.reset(m0)

    def mixer(self):
        P, d = self.P, self.dram
        reg = self.reg
        m0 = reg.mark()
        self.projections()
        reg.reset(m0)
        if not self.debug.get("skip_attn"):
            self.attention()
            reg.reset(m0)
        if not self.debug.get("skip_hyena"):
            self.hyena()
            reg.reset(m0)

    def projections(self):
        P, d = self.P, self.dram
        reg = self.reg
        hT = self.norm_to_hT(1, "mx")
        st = [reg.take(S, BF16, [("ptst", i)]) for i in range(2)]
        slots = {}
        slots[0] = self.issue_gu(d["w_in_t"][0])
        slots[1] = self.issue_gu(d["w_in_t"][1])
        ci = 0
        for pair in range(12):
            if pair + 2 < 12:
                slots[pair + 2] = self.issue_gu(d["w_in_t"][pair + 2])
            si = slots[pair]
            w = self.wgu_slot[si]
            for m in range(2):
                cc = 2 * pair + m
                s_ = st[cc % 2]
                sk = ("ptst", cc % 2)
                for n in range(4):
                    cs = slice(n * 512, (n + 1) * 512)
                    b = ci % 4
                    ci += 1

                    def mm(e, w=w, m=m, cs=cs, b=b):
                        r = None
                        for kc in range(NK):
                            r = e.matmul(self.bank[b][:], lhsT=w[:, m, kc, :], rhs=hT[:, kc, cs], start=(kc == 0), stop=(kc == NK - 1))
                        return r
                    P.op("pe", mm, reads=[("wgu", si)] + [("mxhT", kc, n) for kc in range(NK)], writes=[("bank", b)])
                    self.evac(s_[:, cs], self.bank[b][:], reads=[("bank", b)], writes=[sk])
                P.dma("sp", d["pt"][cc], s_, reads=[sk], writes=[("pt", cc)], slot=f"ptst{cc % 2}")

    def attention(self):
        P, d = self.P, self.dram
        reg = self.reg
        bank = self.bank
        ytok = reg.take(16 * 512, BF16, [("ytok", t) for t in range(16)]).rearrange("p (t c) -> p t c", t=16)
        ss = reg.take(16, F32, [("ass",)])
        m_fin = reg.mark()
        ab = reg.take(8 * 640, BF16, [("abias",)]).rearrange("p (h c) -> p h c", h=8)
        P.dma("sp", ab.rearrange("p h c -> p (h c)"), d["abias"], writes=[("abias",)])
        qs, ks = [], []
        for i in range(2):
            qs.append([reg.take(S, BF16, [("aq", i, hh), ("aqz", i, hh)]) for hh in range(2)])
            ks.append(reg.take(S + 128, BF16, [("ak", i), ("akpad", i)]))
        vp = reg.take(S + 512, BF16, [("av",), ("avpad",)])
        q4 = [reg.take(S, BF16, [("q4", hh), ("q4z", hh)]).rearrange("p (r i) -> p r i", r=4) for hh in range(2)]
        k4 = reg.take(4 * 640, BF16, [("k4",), ("k4pad",)]).rearrange("p (r i) -> p r i", r=4)
        q16 = [reg.take(S, BF16, [("q16", hh), ("q16z", hh)]).rearrange("p (r i) -> p r i", r=16) for hh in range(2)]
        k16 = reg.take(S, BF16, [("k16",)]).rearrange("p (r i) -> p r i", r=16)
        NV = (17, 20, 16)
        va = []
        for pi, nv in enumerate(NV):
            va.append(reg.take(nv * 130, BF16, [("va", pi), ("vaones", pi)]).rearrange("p (j h c) -> p j h c", j=nv, h=2))
        pts = [reg.take(512, BF16, [("apt", i)]) for i in range(4)]
        acc = reg.take(S, F32, [("acc",)])
        rden = reg.take(16, F32, [("rden",)])
        for i in range(2):
            P.op("pool", lambda e, i=i: e.memset(ks[i][:, 0:64], 0.0), writes=[("akpad", i)])
            P.op("pool", lambda e, i=i: e.memset(ks[i][:, 64 + S:128 + S], 0.0), writes=[("akpad", i)])
            P.op("pool", lambda e, i=i: e.memset(qs[i][0][64:128, :], 0.0), writes=[("aqz", i, 0)])
            P.op("pool", lambda e, i=i: e.memset(qs[i][1][0:64, :], 0.0), writes=[("aqz", i, 1)])
        P.op("pool", lambda e: e.memset(q4[0][64:128, :, :], 0.0), writes=[("q4z", 0)])
        P.op("pool", lambda e: e.memset(q4[1][0:64, :, :], 0.0), writes=[("q4z", 1)])
        P.op("pool", lambda e: e.memset(q16[0][64:128, :, :], 0.0), writes=[("q16z", 0)])
        P.op("pool", lambda e: e.memset(q16[1][0:64, :, :], 0.0), writes=[("q16z", 1)])
        P.op("pool", lambda e: e.memset(vp[:, 0:256], 0.0), writes=[("avpad",)])
        P.op("pool", lambda e: e.memset(vp[:, 256 + S:512 + S], 0.0), writes=[("avpad",)])
        P.op("pool", lambda e: e.memset(k4[:, :, 0:64], 0.0), writes=[("k4pad",)])
        P.op("pool", lambda e: e.memset(k4[:, :, 576:640], 0.0), writes=[("k4pad",)])
        for pi in range(3):
            P.op("pool", lambda e, pi=pi: e.memset(va[pi][:, :, :, 64:65], 1.0), writes=[("vaones", pi)])
        P.op("pool", lambda e: e.memset(va[0][0:64, 0:1, :, 64:65], 0.0), writes=[("vaones", 0)])
        P.op("pool", lambda e: e.memset(va[0][64:128, 16:17, :, 64:65], 0.0), writes=[("vaones", 0)])
        for r in range(4):
            P.op("pool", lambda e, r=r: e.memset(va[1][0:64, 5 * r:5 * r + 1, :, 64:65], 0.0), writes=[("vaones", 1)])
            P.op("pool", lambda e, r=r: e.memset(va[1][64:128, 5 * r + 4:5 * r + 5, :, 64:65], 0.0), writes=[("vaones", 1)])

        def load_pair(hp):
            i = hp % 2
            P.dma("sp", qs[i][0][0:64, :], d["pt"][12 + hp][0:64, :], reads=[("pt", 12 + hp)], writes=[("aq", i, 0)], slot=f"aq{i}a")
            P.dma("sp", qs[i][1][64:128, :], d["pt"][12 + hp][64:128, :], reads=[("pt", 12 + hp)], writes=[("aq", i, 1)], slot=f"aq{i}b")
            P.dma("sp", ks[i][:, 64:64 + S], d["pt"][16 + hp], reads=[("pt", 16 + hp)], writes=[("ak", i)], slot=f"ak{i}")
        load_pair(0)
        sb_i = [0]
        pt_i = [0]
        ob_i = [0]
        tb_i = [0]
        for hp in range(4):
            i = hp % 2
            if hp + 1 < 4:
                load_pair(hp + 1)
            kT = ks[i]
            kk = ("ak", i)
            P.dma("sp", vp[:, 256:256 + S], d["pt"][20 + hp], reads=[("pt", 20 + hp)], writes=[("av",)], slot="av")
            for hh in range(2):
                ps_ = slice(64 * hh, 64 * hh + 64)
                P.op("pool", lambda e, hh=hh, ps_=ps_, i=i: e.tensor_copy(out=q4[hh][ps_, :, :], in_=qs[i][hh][ps_, :].rearrange("p (i r) -> p r i", r=4)),
                     reads=[("aq", i, hh)], writes=[("q4", hh)])
                P.op("pool", lambda e, hh=hh, ps_=ps_, i=i: e.tensor_copy(out=q16[hh][ps_, :, :], in_=qs[i][hh][ps_, :].rearrange("p (i r) -> p r i", r=16)),
                     reads=[("aq", i, hh)], writes=[("q16", hh)])
            P.op("pool", lambda e, kT=kT: e.tensor_copy(out=k4[:, :, 64:576], in_=kT[:, 64:64 + S].rearrange("p (i r) -> p r i", r=4)),
                 reads=[kk], writes=[("k4",)])
            P.op("pool", lambda e, kT=kT: e.tensor_copy(out=k16, in_=kT[:, 64:64 + S].rearrange("p (i r) -> p r i", r=16)),
                 reads=[kk], writes=[("k16",)])
            vk = [("av",), ("avpad",)]

            def vsl(start, step):
                return vp[:, start:start + 127 * step + 1:step]
            vsrc = [[(vsl(256 - 64 + 128 * J, 1), vk) for J in range(17)],
                    [(vsl(256 + r + 4 * (128 * J - 64), 4), vk) for r in range(4) for J in range(5)],
                    [(vsl(256 + r, 16), vk) for r in range(16)]]
            for pi in range(3):
                lst = vsrc[pi]
                for g0 in range(0, len(lst), 4):
                    grp = lst[g0:g0 + 4]
                    b = 6 + tb_i[0] % 2
                    tb_i[0] += 1
                    pb_ = bank[b].bitcast(BF16)

                    def tr(e, grp=grp, pb_=pb_):
                        r = None
                        for j, (ap, _) in enumerate(grp):
                            r = e.transpose(pb_[:, j * 128:(j + 1) * 128], ap, self.ident_bf)
                        return r
                    P.op("pe", tr, reads=grp[0][1] + [("ident_bf",)], writes=[("bank", b)])
                    ng = len(grp)
                    self.evac(va[pi][:, g0:g0 + ng, :, 0:64], pb_[:, 0:ng * 128].rearrange("p (j h c) -> p j h c", j=ng, h=2),
                              reads=[("bank", b)], writes=[("va", pi)])
            for hh in range(2):
                h = 2 * hp + hh
                qT = qs[i][hh]
                kq = [("aq", i, hh), ("aqz", i, hh)]
                def run_jobs(jobs, pvs):
                    loc = []
                    gi = 0
                    pi_ = 0
                    while gi < len(jobs):
                        grp = []
                        tot = 0
                        while gi < len(jobs) and tot + jobs[gi][3] <= 512:
                            grp.append((jobs[gi], tot))
                            tot += jobs[gi][3]
                            gi += 1
                        b = sb_i[0] % 3
                        sb_i[0] += 1
                        psl = pt_i[0] % 4
                        pt_i[0] += 1
                        rd = [("abias",), ("ident_bf",)]
                        for (jb, _) in grp:
                            rd += jb[4]

                        def mmj(e, grp=grp, b=b):
                            r = None
                            for (k_ap, q_ap, b_ap, n_, _), off in grp:
                                e.matmul(bank[b][:, off:off + n_], lhsT=k_ap, rhs=q_ap, start=True, stop=False)
                                r = e.matmul(bank[b][:, off:off + n_], lhsT=self.ident_bf, rhs=b_ap, start=False, stop=True)
                            return r
                        P.op("pe", mmj, reads=rd, writes=[("bank", b)])
                        P.op("act", lambda e, b=b, psl=psl, tot=tot: e.activation(out=pts[psl][:, 0:tot], in_=bank[b][:, 0:tot], func=AF.Exp, scale=0.125),
                             reads=[("bank", b)], writes=[("apt", psl)])
                        for (_, off) in grp:
                            loc.append((psl, off))
                        while pi_ < len(pvs) and pvs[pi_][0] < len(loc):
                            pvs[pi_][1](loc)
                            pi_ += 1
                    assert pi_ == len(pvs)

                def pv(contribs, ob, col, rd):
                    def f(e):
                        r = None
                        for ci_, (l_ap, psl, pc) in enumerate(contribs):
                            r = e.matmul(bank[ob][0:65, col:col + 128], lhsT=l_ap, rhs=pts[psl][:, pc:pc + 128],
                                         start=(ci_ == 0), stop=(ci_ == len(contribs) - 1))
                        return r
                    P.op("pe", f, reads=rd + [("apt", psl) for (_, psl, _) in contribs], writes=[("bank", ob)])

                def banded(jobs, vtiles, vkeys, ntile, finish):
                    cur = {}
                    pvs = []
                    for I in range(ntile):
                        def cb(loc, I=I):
                            if I % 4 == 0:
                                cur["ob"] = 4 + ob_i[0] % 2
                                ob_i[0] += 1
                            ob = cur["ob"]
                            c_lo = (loc[I][0], loc[I][1] + (0 if I == 0 else 128))
                            c_hi = (loc[I + 1][0], loc[I + 1][1])
                            pv([(vtiles(I), c_lo[0], c_lo[1]), (vtiles(I + 1), c_hi[0], c_hi[1])], ob, (I % 4) * 128, vkeys)
                            if I % 4 == 3:
                                finish(I // 4, ob)
                        pvs.append((I + 1, cb))
                    run_jobs(jobs, pvs)

                jobs = []
                for J in range(17):
                    k_ap = kT[:, 128 * J:128 * J + 128]
                    if J == 0:
                        q_ap, b_ap, n_ = qT[:, 0:128], ab[:, h, 128:256], 128
                    elif J == 16:
                        q_ap, b_ap, n_ = qT[:, 1920:2048], ab[:, h, 0:128], 128
                    else:
                        q_ap, b_ap, n_ = qT[:, 128 * (J - 1):128 * (J + 1)], ab[:, h, 0:256], 256
                    jobs.append((k_ap, q_ap, b_ap, n_, kq + [kk, ("akpad", i)]))
                banded(jobs, lambda J: va[0][:, J, hh, :], [("va", 0), ("vaones", 0)], 16,
                       lambda g, ob: self.evac(acc[0:65, g * 512:(g + 1) * 512], bank[ob][0:65, :], reads=[("bank", ob)], writes=[("acc",)]))
                accv4 = acc.rearrange("p (i r) -> p r i", r=4)
                for r in range(4):
                    jobs = []
                    for J in range(5):
                        k_ap = k4[:, r, 128 * J:128 * J + 128]
                        if J == 0:
                            q_ap, b_ap, n_ = q4[hh][:, r, 0:128], ab[:, h, 256 + 128:256 + 256], 128
                        elif J == 4:
                            q_ap, b_ap, n_ = q4[hh][:, r, 384:512], ab[:, h, 256:256 + 128], 128
                        else:
                            q_ap, b_ap, n_ = q4[hh][:, r, 128 * (J - 1):128 * (J + 1)], ab[:, h, 256:512], 256
                        jobs.append((k_ap, q_ap, b_ap, n_, [("q4", hh), ("q4z", hh), ("k4",), ("k4pad",)]))
                    banded(jobs, lambda J, r=r: va[1][:, 5 * r + J, hh, :], [("va", 1), ("vaones", 1)], 4,
                           lambda g, ob, r=r: P.op("dve", lambda e: e.tensor_tensor(out=accv4[0:65, r, :], in0=bank[ob][0:65, :], in1=accv4[0:65, r, :], op=ALU.add),
                                                   reads=[("bank", ob), ("acc",)], writes=[("acc",)]))
                accv16 = acc.rearrange("p (m r) -> p m r", r=16)
                for r0 in range(0, 16, 4):
                    jobs = []
                    for r in range(r0, r0 + 4):
                        jobs.append((k16[:, r, :], q16[hh][:, r, :], ab[:, h, 512:640], 128, [("q16", hh), ("q16z", hh), ("k16",)]))
                    ob = 4 + ob_i[0] % 2
                    ob_i[0] += 1
                    pvs = []
                    for j in range(4):
                        def cb(loc, j=j, ob=ob, r0=r0):
                            pv([(va[2][:, r0 + j, hh, :], loc[j][0], loc[j][1])], ob, j * 128, [("va", 2), ("vaones", 2)])
                        pvs.append((j, cb))
                    run_jobs(jobs, pvs)
                    P.op("dve", lambda e, ob=ob, r0=r0: e.tensor_tensor(
                        out=accv16[0:65, :, r0:r0 + 4], in0=bank[ob][0:65, :].rearrange("p (r m) -> p m r", r=4),
                        in1=accv16[0:65, :, r0:r0 + 4], op=ALU.add),
                        reads=[("bank", ob), ("acc",)], writes=[("acc",)])
                if "acc" in self.debug.get("dump", ()) and h == self.debug.get("acc_head", 0):
                    self.dbg("acc", acc[0:65, :], [("acc",)])
                for (t0_, nt) in ((0, 7), (7, 7), (14, 2)):
                    b = 6 + tb_i[0] % 2
                    tb_i[0] += 1

                    def trf(e, t0_=t0_, nt=nt, b=b):
                        r = None
                        for j in range(nt):
                            t = t0_ + j
                            r = e.transpose(bank[b][:, j * 65:(j + 1) * 65], acc[0:65, t * 128:(t + 1) * 128], self.ident_f[0:65, 0:65])
                        return r
                    P.op("pe", trf, reads=[("acc",), ("ident_f",)], writes=[("bank", b)])
                    bv = bank[b][:, 0:nt * 65].rearrange("p (t c) -> p t c", t=nt)
                    P.op("dve", lambda e, bv=bv, t0_=t0_, nt=nt: e.reciprocal(out=rden[:, t0_:t0_ + nt], in_=bv[:, :, 64]),
                         reads=[("bank", b)], writes=[("rden",)])
                    P.op("dve", lambda e, bv=bv, t0_=t0_, nt=nt, h=h: e.tensor_tensor(
                        out=ytok[:, t0_:t0_ + nt, h * 64:(h + 1) * 64], in0=bv[:, :, 0:64],
                        in1=rden[:, t0_:t0_ + nt].unsqueeze(2).broadcast_to([128, nt, 64]), op=ALU.mult),
                        reads=[("bank", b), ("rden",)], writes=[("ytok", t) for t in range(t0_, t0_ + nt)])
        self.dbg("ytok", ytok.rearrange("p t c -> p (t c)"), [("ytok", t) for t in range(16)])
        reg.reset(m_fin)
        junk = reg.take(512, BF16, [("ajunk",)])
        for t in range(16):
            P.op("act", lambda e, t=t: e.activation(out=junk, in_=ytok[:, t, :], func=AF.Square, accum_out=ss[:, t:t + 1]),
                 reads=[("ytok", t)], writes=[("ajunk",), ("ass",)])
        P.op("act", lambda e: e.activation(out=ss, in_=ss, func=AF.Sqrt, bias=self.eps_ap, scale=1.0 / 512), reads=[("ass",), ("eps",)], writes=[("ass",)])
        P.op("dve", lambda e: e.reciprocal(out=ss, in_=ss), reads=[("ass",)], writes=[("ass",)])
        for t in range(16):
            P.op("dve", lambda e, t=t: e.tensor_scalar(out=ytok[:, t, :], in0=ytok[:, t, :], scalar1=ss[:, t:t + 1], scalar2=None, op0=ALU.mult),
                 reads=[("ytok", t), ("ass",)], writes=[("ytok", t)])
        self.to_feature_major(ytok, lambda t: ("ytok", t), 4, lambda cc, st_, sk: P.dma("sp", d["ynat"][cc], st_, reads=[sk], writes=[("ynat", cc)], slot="yn" + sk[0]))

    def to_feature_major(self, tok, tokkey, goff, sink=None, dst=None, dstkey=None):
        P = self.P
        bank = self.bank
        reg = self.reg
        if dst is None:
            stk = [(self.u("fmst"), 0) for _ in range(2)]
            st = [reg.take(S, BF16, [stk[i]]) for i in range(2)]
        for cc in range(4):
            if dst is None:
                s_ = st[cc % 2]
                sk = stk[cc % 2]
            else:
                s_ = dst[:, cc, :]
                sk = (dstkey, cc)
            for g in range(4):
                b = 6 + g % 2
                pb_ = bank[b].bitcast(BF16)

                def tr(e, cc=cc, g=g, pb_=pb_):
                    r = None
                    for j in range(4):
                        t = 4 * g + j
                        r = e.transpose(pb_[:, j * 128:(j + 1) * 128], tok[:, t, cc * 128:(cc + 1) * 128], self.ident_bf)
                    return r
                P.op("pe", tr, reads=[tokkey(t) for t in range(4 * g, 4 * g + 4)] + [("ident_bf",)], writes=[("bank", b)])
                P.op("dve", lambda e, s_=s_, g=g, pb_=pb_, cc=cc: e.tensor_scalar(
                    out=s_[:, g * 512:(g + 1) * 512], in0=pb_[:, 0:512], scalar1=self.gout[:, goff + cc:goff + cc + 1], scalar2=None, op0=ALU.mult),
                    reads=[("bank", b), ("gout",)], writes=[sk])
            if sink is not None:
                sink(cc, s_, sk)

    def hyena(self):
        P, d = self.P, self.dram
        reg = self.reg
        bank = self.bank
        U = [reg.take(16 * 512, BF16, [("hu", g, t) for t in range(16)]).rearrange("p (t c) -> p t c", t=16) for g in range(3)]
        m1 = reg.mark()
        diag = reg.take(36 * 128, BF16, [("diag",)]).rearrange("p (c i j) -> p c i j", c=12, i=3)
        cw = reg.take(36, F32, [("convw",)])
        cbf = reg.take(1536, BF16, [("convb",)])
        one0 = reg.take(128, BF16, [("one0",)])
        P.op("pool", lambda e: e.memset(cbf, 0.0), writes=[("convb",)])
        P.op("pool", lambda e: e.memset(one0, 0.0), writes=[("one0",)])
        P.op("pool", lambda e: e.memset(one0[0:1, :], 1.0), writes=[("one0",)])
        PT = [reg.take(4 * 2064, BF16, [("hpt", i), ("hptpad", i)]).rearrange("p (c t) -> p c t", c=4) for i in range(2)]
        P.dma("sp", cw, d["convw"], writes=[("convw",)])
        P.dma("pool", cbf[0:1, :], d["convb"], writes=[("convb",)])
        for ci in range(36):
            P.op("dve", lambda e, ci=ci: e.tensor_scalar(out=diag[:, ci // 3, ci % 3, :], in0=self.ident_f, scalar1=cw[:, ci:ci + 1], scalar2=None, op0=ALU.mult),
                 reads=[("convw",), ("ident_f",)], writes=[("diag",)])
        for i in range(2):
            P.op("pool", lambda e, i=i: e.memset(PT[i][:, :, 0:1], 0.0), writes=[("hptpad", i)])
            P.op("pool", lambda e, i=i: e.memset(PT[i][:, :, S + 1:S + 2], 0.0), writes=[("hptpad", i)])

        def load_grp(g):
            i = g % 2
            for c in range(4):
                P.dma("sp", PT[i][:, c, 1:S + 1], d["pt"][4 * g + c], reads=[("pt", 4 * g + c)], writes=[("hpt", i)], slot=f"hpt{i}_{c}")
        load_grp(0)
        bi = 0
        for g in range(3):
            if g + 1 < 3:
                load_grp(g + 1)
            i = g % 2
            for tt in range(16):
                b = bi % 4
                bi += 1

                def mmc(e, g=g, tt=tt, b=b, i=i):
                    r = None
                    for c in range(4):
                        o = bank[b][:, c * 128:(c + 1) * 128]
                        e.matmul(o, lhsT=one0, rhs=cbf[:, (4 * g + c) * 128:(4 * g + c + 1) * 128], start=True, stop=False)
                        for tap in range(3):
                            r = e.matmul(o, lhsT=PT[i][:, c, tt * 128 + tap:tt * 128 + tap + 128], rhs=diag[:, 4 * g + c, tap, :],
                                         start=False, stop=(tap == 2))
                    return r
                P.op("pe", mmc, reads=[("hpt", i), ("hptpad", i), ("diag",), ("convb",), ("one0",)], writes=[("bank", b)])
                self.evac(U[g][:, tt, :], bank[b][:], reads=[("bank", b)], writes=[("hu", g, tt)])
        for g in range(3):
            self.dbg(f"hu{g}", U[g].rearrange("p t c -> p (t c)"), [("hu", g, t) for t in range(16)])
        reg.reset(m1)
        Y = reg.take(32 * 512, BF16, [("hy", j) for j in range(32)]).rearrange("p (j c) -> p j c", j=32)
        cb = [reg.take(4096, BF16, [("hcb", i)]).rearrange("p (a j c) -> p a j c", a=2, j=16) for i in range(2)]
        hsl = [reg.take(1024, F32, [("hh", i)]).rearrange("p (a c) -> p a c", a=2) for i in range(2)]
        mt = [reg.take(512, F32, [("hm", i)]) for i in range(4)]
        cbi = [0]
        fb = [0]

        def load_cb(kt):
            si = cbi[0] % 2
            cbi[0] += 1
            P.dma("sp", cb[si].rearrange("p a j c -> p (a j c)"), d["cs_cb"][kt], writes=[("hcb", si)], slot=f"hcb{si}")
            return si

        def conv(o, IN, ink, G, gk, OUT, outk):
            si_next = load_cb(0)
            for kt in range(16):
                si = si_next
                hs_i = kt % 2
                P.dma("sp", hsl[hs_i].rearrange("p a c -> p (a c)"), d["hspec"][o, kt], reads=[("hspec", o, kt)], writes=[("hh", hs_i)], slot=f"hh{hs_i}")
                if kt + 1 < 16:
                    si_next = load_cb(kt + 1)
                ba = 2 * (fb[0] % 2)
                fb[0] += 1
                for part in range(2):
                    def mmf(e, part=part, si=si, ba=ba):
                        r = None
                        for jt in range(16):
                            r = e.matmul(bank[ba + part][:], lhsT=cb[si][:, part, jt, :], rhs=IN[:, jt, :], start=(jt == 0), stop=(jt == 15))
                        return r
                    P.op("pe", mmf, reads=[("hcb", si)] + [(ink[0], ink[1], t) for t in range(16)], writes=[("bank", ba + part)])
                uc, us = bank[ba], bank[ba + 1]
                hre, him = hsl[hs_i][:, 0, :], hsl[hs_i][:, 1, :]
                hk = ("hh", hs_i)
                P.op("dve", lambda e, uc=uc, hre=hre: e.tensor_tensor(out=mt[0], in0=uc[:], in1=hre, op=ALU.mult), reads=[("bank", ba), hk], writes=[("hm", 0)])
                P.op("dve", lambda e, us=us, him=him: e.tensor_tensor(out=mt[1], in0=us[:], in1=him, op=ALU.mult), reads=[("bank", ba + 1), hk], writes=[("hm", 1)])
                P.op("dve", lambda e, us=us, hre=hre: e.tensor_tensor(out=mt[2], in0=us[:], in1=hre, op=ALU.mult), reads=[("bank", ba + 1), hk], writes=[("hm", 2)])
                P.op("dve", lambda e, uc=uc, him=him: e.tensor_tensor(out=mt[3], in0=uc[:], in1=him, op=ALU.mult), reads=[("bank", ba), hk], writes=[("hm", 3)])
                P.op("pool", lambda e, kt=kt: e.tensor_tensor(out=Y[:, 2 * kt, :], in0=mt[0], in1=mt[1], op=ALU.add),
                     reads=[("hm", 0), ("hm", 1)], writes=[("hy", 2 * kt)])
                P.op("pool", lambda e, kt=kt: e.tensor_tensor(out=Y[:, 2 * kt + 1, :], in0=mt[2], in1=mt[3], op=ALU.subtract),
                     reads=[("hm", 2), ("hm", 3)], writes=[("hy", 2 * kt + 1)])
            si_next = load_cb(0)
            for nt in range(16):
                si = si_next
                if nt + 1 < 16:
                    si_next = load_cb(nt + 1)
                b = 4 + nt % 2

                def mmi(e, si=si, b=b):
                    r = None
                    for j in range(32):
                        r = e.matmul(bank[b][:], lhsT=cb[si][:, j % 2, j // 2, :], rhs=Y[:, j, :], start=(j == 0), stop=(j == 31))
                    return r
                P.op("pe", mmi, reads=[("hcb", si)] + [("hy", j) for j in range(32)], writes=[("bank", b)])
                P.op("dve", lambda e, b=b, nt=nt: e.tensor_tensor(out=OUT[:, nt, :], in0=bank[b][:], in1=G[:, nt, :], op=ALU.mult),
                     reads=[("bank", b), (gk[0], gk[1], nt)], writes=[(outk[0], outk[1], nt)])
        conv(0, U[0], ("hu", 0), U[1], ("hu", 1), U[0], ("hu", 0))
        self.dbg("z1", U[0].rearrange("p t c -> p (t c)"), [("hu", 0, t) for t in range(16)])
        conv(1, U[0], ("hu", 0), U[2], ("hu", 2), U[1], ("hu", 1))
        yh = U[1]
        self.dbg("yhy", yh.rearrange("p t c -> p (t c)"), [("hu", 1, t) for t in range(16)])
        reg.reset(m1)
        ss = reg.take(16, F32, [("hss",)])
        junk = reg.take(512, BF16, [("hjunk",)])
        ynh = reg.take(4 * S, BF16, [("ynh", c) for c in range(4)]).rearrange("p (c t) -> p c t", c=4)
        yna = reg.take(4 * S, BF16, [("yna", c) for c in range(4)]).rearrange("p (c t) -> p c t", c=4)
        for c in range(4):
            P.dma("sp", yna[:, c, :], d["ynat"][c], reads=[("ynat", c)], writes=[("yna", c)], slot=f"yna{c}")
        for t in range(16):
            P.op("act", lambda e, t=t: e.activation(out=junk, in_=yh[:, t, :], func=AF.Square, accum_out=ss[:, t:t + 1]),
                 reads=[("hu", 1, t)], writes=[("hjunk",), ("hss",)])
        P.op("act", lambda e: e.activation(out=ss, in_=ss, func=AF.Sqrt, bias=self.eps_ap, scale=1.0 / 512), reads=[("hss",), ("eps",)], writes=[("hss",)])
        P.op("dve", lambda e: e.reciprocal(out=ss, in_=ss), reads=[("hss",)], writes=[("hss",)])
        for t in range(16):
            P.op("dve", lambda e, t=t: e.tensor_scalar(out=yh[:, t, :], in0=yh[:, t, :], scalar1=ss[:, t:t + 1], scalar2=None, op0=ALU.mult),
                 reads=[("hu", 1, t), ("hss",)], writes=[("hu", 1, t)])
        self.to_feature_major(yh, lambda t: ("hu", 1, t), 0, dst=ynh, dstkey="ynh")
        X = self.X
        slots = {0: self.issue_wd(d["w_out_t"][0], 1024)}
        bi = 0
        for dc in range(NK):
            if dc + 1 < NK:
                slots[dc + 1] = self.issue_wd(d["w_out_t"][dc + 1], 1024)
            si = slots[dc]
            w = self.wd_slot[si]
            for n in range(4):
                cs = slice(n * 512, (n + 1) * 512)
                b = bi % 4
                bi += 1

                def mmo(e, w=w, cs=cs, b=b):
                    r = None
                    for cc in range(8):
                        src_ = ynh[:, cc, cs] if cc < 4 else yna[:, cc - 4, cs]
                        r = e.matmul(bank[b][:], lhsT=w[:, cc, :], rhs=src_, start=(cc == 0), stop=(cc == 7))
                    return r
                P.op("pe", mmo, reads=[("wd", si, 0), ("wd", si, 1)] + [("ynh", c) for c in range(4)] + [("yna", c) for c in range(4)],
                     writes=[("bank", b)])
                P.op("dve", lambda e, b=b, dc=dc, cs=cs, n=n: e.tensor_tensor(out=X[:, dc, cs], in0=bank[b][:], in1=X[:, dc, cs], op=ALU.add),
                     reads=[("bank", b), ("X", dc, n)], writes=[("X", dc, n)])

    def build(self):
        P, d = self.P, self.dram
        dbg = self.debug
        self.load_consts()
        if not dbg.get("skip_filter"):
            self.filter_phase()
            if "hspec" in dbg.get("dump", ()):
                o = self.nc.dram_tensor("dbg_hspec", [2, 16, 128, 1024], F32, kind="ExternalOutput").ap()
                self.final_toks.append(P.dma("sp", o, d["hspec"], reads=[("hspec", o_, kt) for o_ in range(2) for kt in range(16)],
                                             slot="dbg_hspec"))
        if not dbg.get("skip_ffn1"):
            self.ffn(1, 0)
        if not dbg.get("skip_mixer"):
            self.mixer()
        if not dbg.get("skip_ffn2"):
            self.ffn(2, 2)
        self.final(3)
        P.emit(self.final_toks)
        P.close()
        return self.nc


def bf16(a):
    return np.asarray(a, np.float32).astype(ml_dtypes.bfloat16)


def make_consts():
    c = {}
    N = 2 * S
    n = np.arange(S, dtype=np.float64)
    th = 2.0 * math.pi * np.outer(n + 0.5, n + 0.5) / N
    Cm, Sm = np.cos(th), np.sin(th)
    cs = np.stack([Cm, Sm], 0).reshape(2, 16, 128, 16, 128)
    c["cs_cb"] = bf16(np.ascontiguousarray(cs.transpose(3, 2, 0, 1, 4)).reshape(16, 128, 4096))
    t = np.linspace(0.0, 1.0, S, dtype=np.float32)[:, None]
    w = (2.0 * math.pi * np.arange(S, dtype=np.float32)[:, None] / S).astype(np.float32)
    f = np.linspace(1e-4, 15, 16, dtype=np.float32)[None, :]
    z = np.concatenate([t, np.cos(f * w), -np.sin(f * w)], -1).astype(np.float32)
    c["zT"] = np.ascontiguousarray(z.T)
    deltas = np.linspace(math.log(1e-2) / 1.5, math.log(1e-2) / 0.3, 512, dtype=np.float32)
    decay = np.exp(-t * np.abs(deltas)[None, :]).astype(np.float32)
    dsh = np.concatenate([decay[1:], np.zeros((1, 512), np.float32)], 0)
    c["decay"] = np.ascontiguousarray(np.concatenate([decay.reshape(16, 128, 512), dsh.reshape(16, 128, 512)], -1))
    k = np.arange(S, dtype=np.float64)
    phi = math.pi * (k + 0.5) / N
    cph = (2.0 / N) * np.cos(phi)
    sph = (2.0 / N) * np.sin(phi)
    rot = np.concatenate([cph.reshape(16, 128).T, sph.reshape(16, 128).T, -sph.reshape(16, 128).T], 1)
    c["rot"] = np.ascontiguousarray(rot).astype(np.float32)
    c["ident"] = np.eye(128, dtype=np.float32)
    p = np.arange(128)[:, None]
    tabs = []
    for h in range(8):
        slope = 2.0 ** (-(h + 1))
        cc = np.arange(256)[None, :]
        dl = 64 + p - cc
        t1 = np.where(np.abs(dl) <= 64, -8.0 * slope * np.abs(dl), -240000.0)
        t2 = np.where(np.abs(dl) <= 64, -8.0 * slope * 4 * np.abs(dl), -240000.0)
        c3 = np.arange(128)[None, :]
        d3 = p - c3
        t3 = np.where(np.abs(d3) <= 64, -8.0 * slope * 16 * np.abs(d3), -240000.0)
        tabs.append(np.concatenate([t1, t2, t3], 1))
    c["abias"] = bf16(np.stack(tabs, 1).reshape(128, 8 * 640))
    return c


def prep_shared(inp):
    sh = dict(make_consts())
    f32 = lambda a: np.ascontiguousarray(np.asarray(a, np.float32))
    g = np.stack([inp["ffn1_norm_g"], inp["mix_norm_g"], inp["ffn2_norm_g"], inp["final_norm_g"]], 0)
    sh["gains"] = f32(g.reshape(4, NK, 128).transpose(2, 0, 1).reshape(128, 4 * NK))
    for i in (1, 2):
        wg = np.asarray(inp[f"ffn{i}_w_gate"], np.float32).reshape(NK, 128, NF, 128)
        wu = np.asarray(inp[f"ffn{i}_w_up"], np.float32).reshape(NK, 128, NF, 128)
        gu = np.stack([wg, wu], 0)
        sh[f"wgu{i}"] = f32(gu.transpose(3, 2, 0, 1, 4).reshape(NF, 128, 2048))
        wd = np.asarray(inp[f"ffn{i}_w_down"], np.float32).reshape(NF, 128, NK, 128)
        sh[f"wd{i}"] = f32(wd.transpose(2, 1, 0, 3).reshape(NK, 128, NF * 128))
    wi = np.asarray(inp["w_in"], np.float32).reshape(NK, 128, 12, 2, 128)
    sh["w_in_t"] = f32(wi.transpose(2, 1, 3, 0, 4).reshape(12, 128, 2048))
    wo = np.asarray(inp["w_out"], np.float32).reshape(NK, 128, NK, 128)
    sh["w_out_t"] = f32(wo.transpose(2, 1, 0, 3).reshape(NK, 128, 1024))
    go = np.concatenate([np.asarray(inp["hy_out_norm_g"]).reshape(4, 128), np.asarray(inp["attn_out_norm_g"]).reshape(4, 128)], 0)
    sh["gout"] = f32(go.T)
    cw = np.asarray(inp["hy_conv_w"], np.float32).reshape(3, 12, 128)
    sh["convw"] = f32(cw.transpose(2, 1, 0).reshape(128, 36))
    sh["convb"] = f32(np.asarray(inp["hy_conv_b"]).reshape(1, 1536))
    sh["filt_small"] = f32(np.stack([inp["hy_filt_b1"], inp["hy_filt_b2"], inp["hy_filt_b3"], inp["hy_filt_freq"]] +
                                    [np.zeros(64, np.float32)] * 4, 1))
    sh["fw1"] = f32(inp["hy_filt_w1"])
    sh["fw23"] = f32(np.concatenate([inp["hy_filt_w2"], inp["hy_filt_w3"]], 1))
    sh["fwout"] = f32(inp["hy_filt_w_out"])
    sh["skip"] = f32(inp["hy_filt_skip"])
    return sh


_CACHE = {}


def kernel(**inputs):
    inp = {k: np.asarray(v) for k, v in inputs.items()}
    x = inp["x"].astype(np.float32)
    sh = prep_shared(inp)
    if "nc" not in _CACHE:
        _CACHE["nc"] = Builder().build()
    nc = _CACHE["nc"]
    in_maps = []
    for b in range(NCORES):
        m = dict(sh)
        m["xT"] = np.ascontiguousarray(x[b].T)
        in_maps.append(m)
    res = run_bass_kernel_spmd(nc, in_maps, core_ids=list(range(NCORES)))
    out = np.stack([np.ascontiguousarray(res.results[b]["outT"].T) for b in range(NCORES)], 0)
    return out.astype(np.float32)
```

```python
from contextlib import ExitStack
import math
import numpy as np
import ml_dtypes
import concourse.bass as bass
import concourse.mybir as mybir
from concourse.bass_utils import run_bass_kernel_spmd

F32 = mybir.dt.float32
BF16 = mybir.dt.bfloat16
AF = mybir.ActivationFunctionType
ALU = mybir.AluOpType
AX = mybir.AxisListType

S = 2048
D = 1024
DFF = 2816
NF = DFF // 128
NK = D // 128
EPS = 1e-6
NCORES = 8


class Prog:
    ENG = ("pe", "act", "dve", "pool", "sp")

    def __init__(self, nc):
        self.nc = nc
        self.es = ExitStack()
        self.sem = {}
        self.cnt = {}
        self.ops = {e: [] for e in self.ENG}
        self.known = {e: {} for e in self.ENG}
        for e in self.ENG:
            self.sem[e] = self.es.enter_context(nc.semaphore("s_" + e))
            self.cnt[e] = 0
        self.slot_sem = {}
        self.slot_cnt = {}
        self.last_w = {}
        self.readers = {}
        self.nwaits = 0

    def sb(self, name, shape, dt):
        return self.es.enter_context(self.nc.sbuf_tensor(name, list(shape), dt))

    def ps(self, name, shape, dt=F32):
        return self.es.enter_context(self.nc.psum_tensor(name, list(shape), dt))

    def _deps(self, reads, writes):
        toks = []
        for k in reads:
            t = self.last_w.get(k)
            if t is not None:
                toks.append(t)
        for k in writes:
            t = self.last_w.get(k)
            if t is not None:
                toks.append(t)
            toks.extend(self.readers.get(k, ()))
        return toks

    def _commit(self, reads, writes, tok):
        for k in reads:
            self.readers.setdefault(k, []).append(tok)
        for k in writes:
            self.last_w[k] = tok
            self.readers[k] = []

    def alias(self, new_keys, old_keys):
        toks = []
        for k in old_keys:
            t = self.last_w.get(k)
            if t is not None:
                toks.append(t)
            toks.extend(self.readers.get(k, ()))
        for k in new_keys:
            self.last_w[k] = None
            self.readers[k] = list(toks)

    def _waits(self, e, toks):
        need = {}
        for (s, v) in toks:
            if s == "pe" and e == "pe":
                continue
            if v > need.get(s, 0):
                need[s] = v
        out = []
        kn = self.known[e]
        for s, v in need.items():
            if kn.get(s, 0) >= v:
                continue
            kn[s] = v
            out.append((s, v))
        self.nwaits += len(out)
        return out

    def op(self, e, fn, reads=(), writes=(), extra=()):
        toks = self._deps(reads, writes) + list(extra)
        waits = self._waits(e, toks)
        self.cnt[e] += 1
        tok = (e, self.cnt[e])
        self.ops[e].append((waits, fn, ("eng", e)))
        self._commit(reads, writes, tok)
        return tok

    def dma(self, q, out, in_, reads=(), writes=(), slot=None, extra=(), **kw):
        if slot is None:
            k0 = writes[0] if writes else reads[0]
            slot = "_".join(str(x) for x in k0)
        if slot not in self.slot_sem:
            self.slot_sem[slot] = self.es.enter_context(self.nc.semaphore("q_" + slot))
            self.slot_cnt[slot] = 0
        toks = self._deps(reads, writes) + list(extra)
        waits = self._waits(q, toks)
        self.slot_cnt[slot] += 16
        tok = ("slot:" + slot, self.slot_cnt[slot])

        def fn(eng, out=out, in_=in_, kw=kw):
            return eng.dma_start(out=out, in_=in_, **kw)
        self.ops[q].append((waits, fn, ("slot", slot)))
        self._commit(reads, writes, tok)
        return tok

    def _semof(self, s):
        if s.startswith("slot:"):
            return self.slot_sem[s[5:]]
        return self.sem[s]

    def emit(self, final_tokens):
        nc = self.nc
        engs = {"pe": "tensor", "act": "scalar", "dve": "vector", "pool": "gpsimd", "sp": "sync"}
        fw = self._waits("sp", list(final_tokens))
        with nc.Block() as block:
            for e in self.ENG:
                def body(eng, e=e):
                    for waits, fn, kind in self.ops[e]:
                        for (s, v) in waits:
                            eng.wait_ge(self._semof(s), v)
                        inst = fn(eng)
                        if kind[0] == "eng":
                            inst.then_inc(self.sem[e], 1)
                        else:
                            inst.then_inc(self.slot_sem[kind[1]], 16)
                    if e == "sp":
                        for (s, v) in fw:
                            eng.wait_ge(self._semof(s), v)
                getattr(block, engs[e])(body)

    def close(self):
        self.es.close()


class Arena:
    def __init__(self, P, name, nbytes):
        self.P = P
        self.t = P.sb(name, [128, nbytes // 2], BF16)
        self.size = nbytes // 2
        self.off = 0
        self.name = name
        self.hist = []

    def take(self, nelem, dt, keys):
        w = nelem * (2 if dt == F32 else 1)
        n2 = (w + 15) // 16 * 16
        assert self.off + n2 <= self.size, f"arena {self.name} overflow need {(self.off + n2) * 2} have {self.size * 2}"
        lo, hi = self.off, self.off + n2
        v = self.t[:, lo:lo + w]
        self.off = hi
        old = []
        for (a, b, ks) in self.hist:
            if a < hi and lo < b:
                old.extend(ks)
        keys = list(keys)
        for k in keys:
            assert k not in self.P.last_w and k not in self.P.readers, f"key reuse {k}"
        self.P.alias(keys, old)
        self.hist.append((lo, hi, keys))
        if dt == F32:
            v = v.bitcast(F32)
        return v

    def take_at(self, lo, nelem, dt, keys):
        w = nelem * (2 if dt == F32 else 1)
        hi = lo + (w + 15) // 16 * 16
        assert hi <= self.size
        v = self.t[:, lo:lo + w]
        old = []
        for (a, b, ks) in self.hist:
            if a < hi and lo < b:
                old.extend(ks)
        keys = list(keys)
        for k in keys:
            assert k not in self.P.last_w and k not in self.P.readers, f"key reuse {k}"
        self.P.alias(keys, old)
        self.hist.append((lo, hi, keys))
        if dt == F32:
            v = v.bitcast(F32)
        return v

    def mark(self):
        return self.off

    def reset(self, m=0):
        self.off = m


class Builder:
    def __init__(self, debug=None):
        self.debug = debug or {}
        nc = bass.Bass("TRN2", target_bir_lowering=False)
        self.nc = nc
        P = Prog(nc)
        self.P = P
        self.final_toks = []
        d = self.dram = {}
        self.uid = 0

        def din(name, shape, dt=F32):
            d[name] = nc.dram_tensor(name, list(shape), dt, kind="ExternalInput").ap()

        def dscr(name, shape, dt):
            d[name] = nc.dram_tensor(name, list(shape), dt, kind="Internal").ap()
        din("xT", [D, S])
        din("gains", [128, 4 * NK])
        din("ident", [128, 128])
        for i in (1, 2):
            din(f"wgu{i}", [NF, 128, 2048])
            din(f"wd{i}", [NK, 128, NF * 128])
        din("w_in_t", [12, 128, 2048])
        din("w_out_t", [NK, 128, 1024])
        din("gout", [128, 8])
        din("convw", [128, 36])
        din("convb", [1, 1536])
        din("abias", [128, 8 * 640], BF16)
        din("cs_cb", [16, 128, 4096], BF16)
        din("zT", [33, S])
        din("decay", [16, 128, 1024])
        din("rot", [128, 48])
        din("filt_small", [64, 8])
        din("fw1", [33, 64])
        din("fw23", [64, 128])
        din("fwout", [64, 2048])
        din("skip", [2, 512])
        dscr("hspec", [2, 16, 128, 1024], F32)
        dscr("pt", [24, 128, S], BF16)
        dscr("ynat", [4, 128, S], BF16)
        d["outT"] = nc.dram_tensor("outT", [D, S], F32, kind="ExternalOutput").ap()

        self.X = P.sb("X", [128, NK, S], F32)
        self.cst = Arena(P, "cst", 2 * 1024)
        self.wsl = Arena(P, "wsl", 23 * 1024)
        self.reg = Arena(P, "reg", 118 * 1024)
        self.HT_OFF = 118 * 512 - NK * S
        c = self.cst
        self.gains = c.take(4 * NK, F32, [("gains",)])
        self.gout = c.take(8, F32, [("gout",)])
        self.eps_ap = c.take(1, F32, [("eps",)])
        self.ones_bf = c.take(128, BF16, [("ones",)])
        self.ident_f = c.take(128, F32, [("ident_f",)])
        self.ident_bf = c.take(128, BF16, [("ident_bf",)])
        self.rot = c.take(48, F32, [("rot",)])
        self.wgu_slot = [self.wsl.take(2048, BF16, [("wgu", i)]).rearrange("p (m k j) -> p m k j", m=2, k=NK) for i in range(3)]
        self.wd_slot = [self.wsl.take(NF * 128, BF16, [("wd", i, 0), ("wd", i, 1)]).rearrange("p (f j) -> p f j", f=NF) for i in range(2)]
        self.bank = [P.ps(f"bank{i}", [128, 512], F32) for i in range(8)]
        self.wgu_i = 0
        self.wd_i = 0
        self.evi = 0

    def u(self, s):
        self.uid += 1
        return f"{s}{self.uid}"

    def dbg(self, name, ap, reads):
        if name not in self.debug.get("dump", ()):
            return
        shape = list(ap.shape)
        o = self.nc.dram_tensor("dbg_" + name, shape, ap.dtype, kind="ExternalOutput").ap()
        self.final_toks.append(self.P.dma("sp", o, ap, reads=reads, slot="dbg_" + name))

    def evac(self, out, in_, reads, writes, scale=None, eng=None):
        P = self.P
        self.evi += 1
        if eng == "act" or (eng is None and self.evi % 2 == 0):
            if scale is None:
                return P.op("act", lambda e: e.activation(out=out, in_=in_, func=AF.Copy), reads=reads, writes=writes)
            return P.op("act", lambda e: e.activation(out=out, in_=in_, func=AF.Copy, scale=scale), reads=reads, writes=writes)
        if scale is None:
            return P.op("dve", lambda e: e.tensor_copy(out=out, in_=in_), reads=reads, writes=writes)
        return P.op("dve", lambda e: e.tensor_scalar(out=out, in0=in_, scalar1=scale, scalar2=None, op0=ALU.mult),
                    reads=reads, writes=writes)

    def load_consts(self):
        P, d = self.P, self.dram
        P.dma("sp", self.gains, d["gains"], writes=[("gains",)])
        P.dma("sp", self.gout, d["gout"], writes=[("gout",)])
        P.dma("sp", self.ident_f, d["ident"], writes=[("ident_f",)])
        P.dma("sp", self.rot, d["rot"], writes=[("rot",)])
        P.op("dve", lambda e: e.memset(self.ones_bf, 1.0), writes=[("ones",)])
        P.op("dve", lambda e: e.memset(self.eps_ap, EPS), writes=[("eps",)])
        P.op("dve", lambda e: e.tensor_copy(out=self.ident_bf, in_=self.ident_f), reads=[("ident_f",)], writes=[("ident_bf",)])
        for kc in range(NK):
            P.dma("sp", self.X[:, kc, :], d["xT"][kc * 128:(kc + 1) * 128, :], writes=[("X", kc, n) for n in range(4)],
                  slot=f"X{kc}")

    def alloc_sq(self, tag):
        return [self.reg.take(512, BF16, [(tag + "sq", i)]) for i in range(4)], tag + "sq"

    def rmsnorm(self, gidx, emit_out, sqp, chunks=(0, 1, 2, 3)):
        P = self.P
        X = self.X
        sq, sqk = sqp
        psN, psR = self.bank[6], self.bank[7]
        qi = 0
        for n in chunks:
            cs = slice(n * 512, (n + 1) * 512)
            for kc in range(NK):
                s = sq[qi % 4]
                sk = (sqk, qi % 4)
                qi += 1
                P.op("act", lambda e, s=s, kc=kc, cs=cs: e.activation(out=s, in_=X[:, kc, cs], func=AF.Square),
                     reads=[("X", kc, n)], writes=[sk])
                P.op("pe", lambda e, s=s, kc=kc: e.matmul(psN[:], lhsT=self.ones_bf, rhs=s, start=(kc == 0), stop=(kc == NK - 1)),
                     reads=[sk, ("ones",)], writes=[("bank", 6)])
            P.op("act", lambda e: e.activation(out=psR[:], in_=psN[:], func=AF.Sqrt, bias=self.eps_ap, scale=1.0 / D),
                 reads=[("bank", 6), ("eps",)], writes=[("bank", 7)])
            P.op("dve", lambda e: e.reciprocal(out=psR[:], in_=psR[:]), reads=[("bank", 7)], writes=[("bank", 7)])
            for kc in range(NK):
                def make(o, kc=kc, cs=cs):
                    return lambda e: e.scalar_tensor_tensor(
                        out=o, in0=X[:, kc, cs], scalar=self.gains[:, gidx * NK + kc:gidx * NK + kc + 1], in1=psR[:],
                        op0=ALU.mult, op1=ALU.mult)
                emit_out(kc, n, cs, make)

    def new_hT(self, tag):
        v = self.reg.take_at(self.HT_OFF, NK * S, BF16, [(tag + "hT", kc, n) for kc in range(NK) for n in range(4)])
        return v.rearrange("p (k t) -> p k t", k=NK)

    def norm_to_hT(self, gidx, hT, tag, sqp, chunks=(0, 1, 2, 3)):
        P = self.P

        def emit(kc, n, cs, make):
            P.op("dve", make(hT[:, kc, cs]), reads=[("X", kc, n), ("bank", 7), ("gains",)], writes=[(tag + "hT", kc, n)])
        self.rmsnorm(gidx, emit, sqp, chunks)

    def issue_gu(self, src):
        si = self.wgu_i % 3
        self.wgu_i += 1
        self.P.dma("pool", self.wgu_slot[si].rearrange("p m k j -> p (m k j)"), src, writes=[("wgu", si)], slot=f"wgu{si}")
        return si

    def issue_wd(self, src, n=NF * 128):
        si = self.wd_i % 2
        self.wd_i += 1
        dst = self.wd_slot[si].rearrange("p f j -> p (f j)")
        half = n // 2
        self.P.dma("pool", dst[:, 0:half], src[:, 0:half], writes=[("wd", si, 0)], slot=f"wd{si}a")
        self.P.dma("pool", dst[:, half:n], src[:, half:n], writes=[("wd", si, 1)], slot=f"wd{si}b")
        return si

    def ffn(self, idx, hT, tag, mid_hook=None):
        P, d = self.P, self.dram
        X = self.X
        reg = self.reg
        m0 = reg.mark()
        AT = reg.take(NF * 1024, BF16, [(tag + "AT", f, n) for f in range(NF) for n in range(2)]).rearrange("p (f t) -> p f t", f=NF)
        sg = [reg.take(512, F32, [(tag + "sg", i)]) for i in range(2)]
        wgu, wd = d[f"wgu{idx}"], d[f"wd{idx}"]
        gu_jobs = [(h, f) for h in range(2) for f in range(NF)]
        wd_jobs = [(h, dc) for h in range(2) for dc in range(NK)]
        gslot = {}
        dslot = {}

        def igu(j):
            if j < len(gu_jobs):
                gslot[j] = self.issue_gu(wgu[gu_jobs[j][1]])

        def iwd(j):
            if j < len(wd_jobs):
                dslot[j] = self.issue_wd(wd[wd_jobs[j][1]])
        igu(0)
        igu(1)
        iwd(0)
        gj = wj = ci = di = 0
        psG = [self.bank[0], self.bank[1]]
        psU = [self.bank[2], self.bank[3]]
        psD = [self.bank[4], self.bank[5]]
        for h in range(2):
            for f in range(NF):
                igu(gj + 2)
                si = gslot[gj]
                gj += 1
                w = self.wgu_slot[si]
                for n2 in range(2):
                    n = h * 2 + n2
                    cs = slice(n * 512, (n + 1) * 512)
                    b = ci % 2
                    ci += 1

                    def mm(e, m, ps, w=w, cs=cs):
                        r = None
                        for kc in range(NK):
                            r = e.matmul(ps[:], lhsT=w[:, m, kc, :], rhs=hT[:, kc, cs], start=(kc == 0), stop=(kc == NK - 1))
                        return r
                    rd = [("wgu", si)] + [(tag + "hT", kc, n) for kc in range(NK)]
                    P.op("pe", lambda e, mm=mm, b=b: mm(e, 0, psG[b]), reads=rd, writes=[("bank", b)])
                    P.op("pe", lambda e, mm=mm, b=b: mm(e, 1, psU[b]), reads=rd, writes=[("bank", 2 + b)])
                    P.op("act", lambda e, b=b: e.activation(out=sg[b], in_=psG[b][:], func=AF.Silu),
                         reads=[("bank", b)], writes=[(tag + "sg", b)])
                    P.op("dve", lambda e, b=b, f=f, n2=n2: e.tensor_tensor(
                        out=AT[:, f, n2 * 512:(n2 + 1) * 512], in0=sg[b], in1=psU[b][:], op=ALU.mult),
                        reads=[(tag + "sg", b), ("bank", 2 + b)], writes=[(tag + "AT", f, n2)])
            for dc in range(NK):
                iwd(wj + 1)
                si = dslot[wj]
                wj += 1
                w = self.wd_slot[si]
                for n2 in range(2):
                    n = h * 2 + n2
                    cs = slice(n * 512, (n + 1) * 512)
                    b = di % 2
                    di += 1

                    def mmd(e, w=w, n2=n2, b=b):
                        r = None
                        for f in range(NF):
                            r = e.matmul(psD[b][:], lhsT=w[:, f, :], rhs=AT[:, f, n2 * 512:(n2 + 1) * 512],
                                         start=(f == 0), stop=(f == NF - 1))
                        return r
                    P.op("pe", mmd, reads=[("wd", si, 0), ("wd", si, 1)] + [(tag + "AT", f, n2) for f in range(NF)],
                         writes=[("bank", 4 + b)])
                    P.op("dve", lambda e, b=b, dc=dc, cs=cs: e.scalar_tensor_tensor(
                        out=X[:, dc, cs], in0=psD[b][:], scalar=0.5, in1=X[:, dc, cs], op0=ALU.mult, op1=ALU.add),
                        reads=[("bank", 4 + b), ("X", dc, n)], writes=[("X", dc, n)])
            if h == 0 and mid_hook is not None:
                mid_hook()
        reg.reset(m0)

    def final(self, gidx, chunks):
        P, d = self.P, self.dram
        reg = self.reg
        if not hasattr(self, "_fin"):
            ost = [reg.take(512, F32, [("ost", i)]) for i in range(4)]
            self._fin = (ost, self.alloc_sq("fin"), [0])
        ost, sqp, oi = self._fin

        def emit(kc, n, cs, make):
            i = oi[0] % 4
            oi[0] += 1
            P.op("dve", make(ost[i]), reads=[("X", kc, n), ("bank", 7), ("gains",)], writes=[("ost", i)])
            self.final_toks.append(P.dma("sp", d["outT"][kc * 128:(kc + 1) * 128, cs], ost[i], reads=[("ost", i)], slot=f"out{i}"))
        self.rmsnorm(gidx, emit, sqp, chunks)

    def filter_phase(self):
        P, d = self.P, self.dram
        reg = self.reg
        m0 = reg.mark()
        TWO_PI = 2.0 * math.pi
        MAGIC = 12582912.0
        AB = reg.take(2 * 2 * 16 * 512, BF16, [("AB", o, ab, J) for o in range(2) for ab in range(2) for J in range(16)]
                      ).rearrange("p (o a j c) -> p o a j c", o=2, a=2, j=16)
        h3 = reg.take(S + 16, BF16, [("h3", n) for n in range(4)] + [("h3pad",)])
        wob = reg.take(2048, BF16, [("wob",)])
        m1 = reg.mark()
        zT = reg.take(S, F32, [("zT",)])
        fw1 = reg.take(128, F32, [("fw1",)])
        fw2 = reg.take(128, F32, [("fw2",)])
        fw3 = reg.take(128, F32, [("fw3",)])
        fsm = reg.take(8, F32, [("fsm",)])
        fsc = reg.take(4, F32, [("fsc",)])
        hA = reg.take(S, F32, [("hA", n) for n in range(4)])
        hB = reg.take(S, F32, [("hB", n) for n in range(4)])
        tb = [reg.take(512, F32, [("ftmp", i)]) for i in range(3)]
        for ap, k in ((zT, ("zT",)), (fw1, ("fw1",)), (fw2, ("fw2",)), (fw3, ("fw3",)), (fsm, ("fsm",)), (wob, ("wob",))):
            P.op("pool", lambda e, ap=ap: e.memset(ap, 0.0), writes=[k])
        P.dma("sp", zT[0:33, :], d["zT"], writes=[("zT",)])
        P.dma("sp", fw1[0:33, 0:64], d["fw1"], writes=[("fw1",)])
        P.dma("sp", fw2[0:64, 0:64], d["fw23"][:, 0:64], writes=[("fw2",)])
        P.dma("sp", fw3[0:64, 0:64], d["fw23"][:, 64:128], writes=[("fw3",)])
        P.dma("sp", fsm[0:64, :], d["filt_small"], writes=[("fsm",)])
        P.dma("pool", wob[0:64, :], d["fwout"], writes=[("wob",)])
        P.op("dve", lambda e: e.memset(h3[:, S:S + 16], 0.0), writes=[("h3pad",)])
        P.op("dve", lambda e: e.tensor_scalar(out=fsc[:, 0:1], in0=fsm[:, 3:4], scalar1=1.0 / TWO_PI, scalar2=None, op0=ALU.mult),
             reads=[("fsm",)], writes=[("fsc",)])
        for l in range(3):
            P.op("dve", lambda e, l=l: e.tensor_tensor(out=fsc[:, 1 + l:2 + l], in0=fsm[:, l:l + 1], in1=fsc[:, 0:1], op=ALU.mult),
                 reads=[("fsm",), ("fsc",)], writes=[("fsc",)])
        layers = [(fw1, ("fw1",), zT, ("zT",), hA, "hA"),
                  (fw2, ("fw2",), hA, "hA", hB, "hB"),
                  (fw3, ("fw3",), hB, "hB", h3, "h3")]
        bi = 0
        for l, (w, wk, src_, srck, dst, dstk) in enumerate(layers):
            for n in range(4):
                cs = slice(n * 512, (n + 1) * 512)
                b = bi % 4
                bi += 1
                ps = self.bank[b]
                rk = [srck] if isinstance(srck, tuple) else [(srck, n)]
                P.op("pe", lambda e, w=w, src_=src_, cs=cs, ps=ps: e.matmul(ps[:], lhsT=w, rhs=src_[:, cs], start=True, stop=True),
                     reads=rk + [wk], writes=[("bank", b)])
                P.op("dve", lambda e, ps=ps, l=l: e.tensor_scalar(out=tb[0], in0=ps[:], scalar1=fsc[:, 0:1],
                                                                scalar2=fsc[:, 1 + l:2 + l], op0=ALU.mult, op1=ALU.add),
                     reads=[("bank", b), ("fsc",)], writes=[("ftmp", 0)])
                P.op("dve", lambda e: e.tensor_scalar(out=tb[1], in0=tb[0], scalar1=MAGIC, scalar2=MAGIC,
                                                      op0=ALU.add, op1=ALU.subtract),
                     reads=[("ftmp", 0)], writes=[("ftmp", 1)])
                P.op("dve", lambda e: e.tensor_tensor(out=tb[2], in0=tb[0], in1=tb[1], op=ALU.subtract),
                     reads=[("ftmp", 0), ("ftmp", 1)], writes=[("ftmp", 2)])
                P.op("act", lambda e, dst=dst, cs=cs: e.activation(out=dst[:, cs], in_=tb[2], func=AF.Sin, scale=6.283185),
                     reads=[("ftmp", 2)], writes=[(dstk, n)])
        self.dbg("h3", h3[0:64, 0:S], [("h3", n) for n in range(4)])
        reg.reset(m1)
        dec = [reg.take(1024, F32, [("dec", i)]).rearrange("p (a c) -> p a c", a=2) for i in range(2)]
        mt = [[[reg.take(512, F32, [("mt", jp, o, i)]) for i in range(2)] for o in range(2)] for jp in range(2)]
        h3k = [("h3", n) for n in range(4)] + [("h3pad",)]
        for J in range(16):
            jp = J % 2
            ds = dec[jp]
            P.dma("sp", ds.rearrange("p a c -> p (a c)"), d["decay"][J], writes=[("dec", jp)], slot=f"dec{jp}")
            for od in range(4):
                off = J * 128 + (od % 2)
                bk = 4 * jp + od
                P.op("pe", lambda e, od=od, off=off, bk=bk: e.matmul(self.bank[bk][:], lhsT=h3[:, off:off + 128], rhs=wob[:, od * 512:(od + 1) * 512],
                                                                     start=True, stop=True),
                     reads=h3k + [("wob",)], writes=[("bank", bk)])
            for o in range(2):
                m_ = mt[jp][o]
                b0 = 4 * jp + 2 * o
                P.op("dve", lambda e, b0=b0, ds=ds, m_=m_: e.tensor_tensor(out=m_[0], in0=self.bank[b0][:], in1=ds[:, 0, :], op=ALU.mult),
                     reads=[("bank", b0), ("dec", jp)], writes=[("mt", jp, o, 0)])
                P.op("dve", lambda e, b0=b0, ds=ds, m_=m_: e.tensor_tensor(out=m_[1], in0=self.bank[b0 + 1][:], in1=ds[:, 1, :], op=ALU.mult),
                     reads=[("bank", b0 + 1), ("dec", jp)], writes=[("mt", jp, o, 1)])
                P.op("pool", lambda e, o=o, J=J, m_=m_: e.tensor_tensor(out=AB[:, o, 0, J, :], in0=m_[0], in1=m_[1], op=ALU.add),
                     reads=[("mt", jp, o, 0), ("mt", jp, o, 1)], writes=[("AB", o, 0, J)])
                P.op("pool", lambda e, o=o, J=J, m_=m_: e.tensor_tensor(out=AB[:, o, 1, J, :], in0=m_[1], in1=m_[0], op=ALU.subtract),
                     reads=[("mt", jp, o, 0), ("mt", jp, o, 1)], writes=[("AB", o, 1, J)])
        reg.reset(m1)
        cb = [reg.take(4096, BF16, [("fcb", i)]).rearrange("p (a j c) -> p a j c", a=2, j=16) for i in range(2)]
        hst = [reg.take(1024, F32, [("hst", i)]).rearrange("p (a c) -> p a c", a=2) for i in range(2)]
        skb = reg.take(1024, F32, [("skb",)]).rearrange("p (o c) -> p o c", o=2)
        t12 = [reg.take(512, F32, [("t12", i)]) for i in range(2)]
        P.dma("sp", skb.rearrange("p o c -> p (o c)"), d["skip"].rearrange("(x o) c -> x (o c)", x=1).broadcast_to([128, 1024]),
              writes=[("skb",)])
        P.op("dve", lambda e: e.tensor_scalar(out=skb.rearrange("p o c -> p (o c)"), in0=skb.rearrange("p o c -> p (o c)"),
                                              scalar1=2.0 / 4096.0, scalar2=None, op0=ALU.mult), reads=[("skb",)], writes=[("skb",)])
        rot = self.rot
        hi = 0
        for kt in range(16):
            c_ = cb[kt % 2]
            P.dma("sp", c_.rearrange("p a j c -> p (a j c)"), d["cs_cb"][kt], writes=[("fcb", kt % 2)], slot=f"fcb{kt % 2}")
            for o in range(2):
                ba, bb = 2 * o, 2 * o + 1
                for part, bk in ((0, ba), (1, bb)):
                    def mmf(e, part=part, bk=bk, c_=c_, o=o):
                        r = None
                        for J in range(16):
                            r = e.matmul(self.bank[bk][:], lhsT=c_[:, part, J, :], rhs=AB[:, o, part, J, :], start=(J == 0), stop=(J == 15))
                        return r
                    P.op("pe", mmf, reads=[("fcb", kt % 2)] + [("AB", o, part, J) for J in range(16)], writes=[("bank", bk)])
                hs = hst[hi % 2]
                hk = ("hst", hi % 2)
                hi += 1
                pre, pim = self.bank[ba], self.bank[bb]
                P.op("dve", lambda e, pre=pre, o=o, kt=kt: e.scalar_tensor_tensor(out=t12[0], in0=pre[:], scalar=rot[:, kt:kt + 1], in1=skb[:, o, :],
                                                                                 op0=ALU.mult, op1=ALU.add),
                     reads=[("bank", ba), ("skb",), ("rot",)], writes=[("t12", 0)])
                P.op("dve", lambda e, pim=pim, hs=hs, kt=kt: e.scalar_tensor_tensor(out=hs[:, 0, :], in0=pim[:], scalar=rot[:, 32 + kt:33 + kt], in1=t12[0],
                                                                                   op0=ALU.mult, op1=ALU.add),
                     reads=[("bank", bb), ("t12", 0), ("rot",)], writes=[hk])
                P.op("dve", lambda e, pre=pre, kt=kt: e.tensor_scalar(out=t12[1], in0=pre[:], scalar1=rot[:, 16 + kt:17 + kt], scalar2=None, op0=ALU.mult),
                     reads=[("bank", ba), ("rot",)], writes=[("t12", 1)])
                P.op("dve", lambda e, pim=pim, hs=hs, kt=kt: e.scalar_tensor_tensor(out=hs[:, 1, :], in0=pim[:], scalar=rot[:, kt:kt + 1], in1=t12[1],
                                                                                   op0=ALU.mult, op1=ALU.add),
                     reads=[("bank", bb), ("t12", 1), ("rot",), hk], writes=[hk])
                P.dma("pool", d["hspec"][o, kt], hs.rearrange("p a c -> p (a c)"), reads=[hk], writes=[("hspec", o, kt)], slot=f"hsp{hi % 2}")
        reg.reset(m0)

    def mixer(self, hT, tag, sqp):
        P, d = self.P, self.dram
        reg = self.reg
        m0 = reg.mark()
        self.projections(hT, tag, sqp)
        reg.reset(m0)
        if not self.debug.get("skip_attn"):
            self.attention()
            reg.reset(m0)
        if not self.debug.get("skip_hyena"):
            self.hyena()
            reg.reset(m0)

    def projections(self, hT, tag, sqp):
        P, d = self.P, self.dram
        reg = self.reg
        self.norm_to_hT(1, hT, tag, sqp, self.mx_chunks_left)
        st = [reg.take(S, BF16, [("ptst", i)]) for i in range(2)]
        slots = {}
        slots[0] = self.issue_gu(d["w_in_t"][0])
        slots[1] = self.issue_gu(d["w_in_t"][1])
        ci = 0
        for pair in range(12):
            if pair + 2 < 12:
                slots[pair + 2] = self.issue_gu(d["w_in_t"][pair + 2])
            si = slots[pair]
            w = self.wgu_slot[si]
            for m in range(2):
                cc = 2 * pair + m
                s_ = st[cc % 2]
                sk = ("ptst", cc % 2)
                for n in range(4):
                    cs = slice(n * 512, (n + 1) * 512)
                    b = ci % 4
                    ci += 1

                    def mm(e, w=w, m=m, cs=cs, b=b):
                        r = None
                        for kc in range(NK):
                            r = e.matmul(self.bank[b][:], lhsT=w[:, m, kc, :], rhs=hT[:, kc, cs], start=(kc == 0), stop=(kc == NK - 1))
                        return r
                    P.op("pe", mm, reads=[("wgu", si)] + [(tag + "hT", kc, n) for kc in range(NK)], writes=[("bank", b)])
                    self.evac(s_[:, cs], self.bank[b][:], reads=[("bank", b)], writes=[sk])
                P.dma("sp", d["pt"][cc], s_, reads=[sk], writes=[("pt", cc)], slot=f"ptst{cc % 2}")

    def attention(self):
        P, d = self.P, self.dram
        reg = self.reg
        bank = self.bank
        ytok = reg.take(16 * 512, BF16, [("ytok", t) for t in range(16)]).rearrange("p (t c) -> p t c", t=16)
        ss = reg.take(16, F32, [("ass",)])
        m_fin = reg.mark()
        ab = reg.take(8 * 640, BF16, [("abias",)]).rearrange("p (h c) -> p h c", h=8)
        P.dma("sp", ab.rearrange("p h c -> p (h c)"), d["abias"], writes=[("abias",)])
        qs, ks = [], []
        for i in range(2):
            qs.append([reg.take(S, BF16, [("aq", i, hh), ("aqz", i, hh)]) for hh in range(2)])
            ks.append(reg.take(S + 128, BF16, [("ak", i), ("akpad", i)]))
        vp = reg.take(S + 512, BF16, [("av",), ("avpad",)])
        q4 = [reg.take(S, BF16, [("q4", hh), ("q4z", hh)]).rearrange("p (r i) -> p r i", r=4) for hh in range(2)]
        k4 = reg.take(4 * 640, BF16, [("k4",), ("k4pad",)]).rearrange("p (r i) -> p r i", r=4)
        q16 = [reg.take(S, BF16, [("q16", hh), ("q16z", hh)]).rearrange("p (r i) -> p r i", r=16) for hh in range(2)]
        k16 = reg.take(S, BF16, [("k16",)]).rearrange("p (r i) -> p r i", r=16)
        NV = (17, 20, 16)
        va = []
        for pi, nv in enumerate(NV):
            va.append(reg.take(nv * 130, BF16, [("va", pi), ("vaones", pi)]).rearrange("p (j h c) -> p j h c", j=nv, h=2))
        pts = [reg.take(512, BF16, [("apt", i)]) for i in range(6)]
        acc = reg.take(S, F32, [("acc",)])
        rden = reg.take(16, F32, [("rden",)])
        for i in range(2):
            P.op("pool", lambda e, i=i: e.memset(ks[i][:, 0:64], 0.0), writes=[("akpad", i)])
            P.op("pool", lambda e, i=i: e.memset(ks[i][:, 64 + S:128 + S], 0.0), writes=[("akpad", i)])
            P.op("pool", lambda e, i=i: e.memset(qs[i][0][64:128, :], 0.0), writes=[("aqz", i, 0)])
            P.op("pool", lambda e, i=i: e.memset(qs[i][1][0:64, :], 0.0), writes=[("aqz", i, 1)])
        P.op("pool", lambda e: e.memset(q4[0][64:128, :, :], 0.0), writes=[("q4z", 0)])
        P.op("pool", lambda e: e.memset(q4[1][0:64, :, :], 0.0), writes=[("q4z", 1)])
        P.op("pool", lambda e: e.memset(q16[0][64:128, :, :], 0.0), writes=[("q16z", 0)])
        P.op("pool", lambda e: e.memset(q16[1][0:64, :, :], 0.0), writes=[("q16z", 1)])
        P.op("pool", lambda e: e.memset(vp[:, 0:256], 0.0), writes=[("avpad",)])
        P.op("pool", lambda e: e.memset(vp[:, 256 + S:512 + S], 0.0), writes=[("avpad",)])
        P.op("pool", lambda e: e.memset(k4[:, :, 0:64], 0.0), writes=[("k4pad",)])
        P.op("pool", lambda e: e.memset(k4[:, :, 576:640], 0.0), writes=[("k4pad",)])
        for pi in range(3):
            P.op("pool", lambda e, pi=pi: e.memset(va[pi][:, :, :, 64:65], 1.0), writes=[("vaones", pi)])
        P.op("pool", lambda e: e.memset(va[0][0:64, 0:1, :, 64:65], 0.0), writes=[("vaones", 0)])
        P.op("pool", lambda e: e.memset(va[0][64:128, 16:17, :, 64:65], 0.0), writes=[("vaones", 0)])
        for r in range(4):
            P.op("pool", lambda e, r=r: e.memset(va[1][0:64, 5 * r:5 * r + 1, :, 64:65], 0.0), writes=[("vaones", 1)])
            P.op("pool", lambda e, r=r: e.memset(va[1][64:128, 5 * r + 4:5 * r + 5, :, 64:65], 0.0), writes=[("vaones", 1)])

        def load_pair(hp):
            i = hp % 2
            P.dma("sp", qs[i][0][0:64, :], d["pt"][12 + hp][0:64, :], reads=[("pt", 12 + hp)], writes=[("aq", i, 0)], slot=f"aq{i}a")
            P.dma("sp", qs[i][1][64:128, :], d["pt"][12 + hp][64:128, :], reads=[("pt", 12 + hp)], writes=[("aq", i, 1)], slot=f"aq{i}b")
            P.dma("sp", ks[i][:, 64:64 + S], d["pt"][16 + hp], reads=[("pt", 16 + hp)], writes=[("ak", i)], slot=f"ak{i}")
        load_pair(0)
        pending = []
        gidx_ = [0]
        sb_i = [0]
        pt_i = [0]
        ob_i = [0]
        tb_i = [0]
        for hp in range(4):
            i = hp % 2
            if hp + 1 < 4:
                load_pair(hp + 1)
            kT = ks[i]
            kk = ("ak", i)
            P.dma("sp", vp[:, 256:256 + S], d["pt"][20 + hp], reads=[("pt", 20 + hp)], writes=[("av",)], slot="av")
            for hh in range(2):
                ps_ = slice(64 * hh, 64 * hh + 64)
                P.op("dve", lambda e, hh=hh, ps_=ps_, i=i: e.tensor_copy(out=q4[hh][ps_, :, :], in_=qs[i][hh][ps_, :].rearrange("p (i r) -> p r i", r=4)),
                     reads=[("aq", i, hh)], writes=[("q4", hh)])
                P.op("dve", lambda e, hh=hh, ps_=ps_, i=i: e.tensor_copy(out=q16[hh][ps_, :, :], in_=qs[i][hh][ps_, :].rearrange("p (i r) -> p r i", r=16)),
                     reads=[("aq", i, hh)], writes=[("q16", hh)])
            P.op("dve", lambda e, kT=kT: e.tensor_copy(out=k4[:, :, 64:576], in_=kT[:, 64:64 + S].rearrange("p (i r) -> p r i", r=4)),
                 reads=[kk], writes=[("k4",)])
            P.op("dve", lambda e, kT=kT: e.tensor_copy(out=k16, in_=kT[:, 64:64 + S].rearrange("p (i r) -> p r i", r=16)),
                 reads=[kk], writes=[("k16",)])
            vk = [("av",), ("avpad",)]

            def vsl(start, step):
                return vp[:, start:start + 127 * step + 1:step]
            vsrc = [[(vsl(256 - 64 + 128 * J, 1), vk) for J in range(17)],
                    [(vsl(256 + r + 4 * (128 * J - 64), 4), vk) for r in range(4) for J in range(5)],
                    [(vsl(256 + r, 16), vk) for r in range(16)]]
            for pi in range(3):
                lst = vsrc[pi]
                for g0 in range(0, len(lst), 4):
                    grp = lst[g0:g0 + 4]
                    b = 6 + tb_i[0] % 2
                    tb_i[0] += 1
                    pb_ = bank[b].bitcast(BF16)

                    def tr(e, grp=grp, pb_=pb_):
                        r = None
                        for j, (ap, _) in enumerate(grp):
                            r = e.transpose(pb_[:, j * 128:(j + 1) * 128], ap, self.ident_bf)
                        return r
                    P.op("pe", tr, reads=grp[0][1] + [("ident_bf",)], writes=[("bank", b)])
                    ng = len(grp)
                    self.evac(va[pi][:, g0:g0 + ng, :, 0:64], pb_[:, 0:ng * 128].rearrange("p (j h c) -> p j h c", j=ng, h=2),
                              reads=[("bank", b)], writes=[("va", pi)], eng="dve")
            for hh in range(2):
                h = 2 * hp + hh
                qT = qs[i][hh]
                kq = [("aq", i, hh), ("aqz", i, hh)]
                def flush_pending(allp=False):
                    while pending and (allp or pending[0][2] <= gidx_[0] - 2):
                        cb_, loc_, _ = pending.pop(0)
                        cb_(loc_)

                def run_jobs(jobs, pvs):
                    loc = []
                    gi = 0
                    pi_ = 0
                    while gi < len(jobs):
                        grp = []
                        tot = 0
                        while gi < len(jobs) and tot + jobs[gi][3] <= 512:
                            grp.append((jobs[gi], tot))
                            tot += jobs[gi][3]
                            gi += 1
                        b = sb_i[0] % 4
                        sb_i[0] += 1
                        psl = pt_i[0] % 6
                        pt_i[0] += 1
                        gidx_[0] += 1
                        rd = [("abias",), ("ident_bf",)]
                        for (jb, _) in grp:
                            rd += jb[4]

                        def mmj(e, grp=grp, b=b):
                            r = None
                            for (k_ap, q_ap, b_ap, n_, _), off in grp:
                                e.matmul(bank[b][:, off:off + n_], lhsT=k_ap, rhs=q_ap, start=True, stop=False)
                                r = e.matmul(bank[b][:, off:off + n_], lhsT=self.ident_bf, rhs=b_ap, start=False, stop=True)
                            return r
                        P.op("pe", mmj, reads=rd, writes=[("bank", b)])
                        P.op("act", lambda e, b=b, psl=psl, tot=tot: e.activation(out=pts[psl][:, 0:tot], in_=bank[b][:, 0:tot], func=AF.Exp, scale=0.125),
                             reads=[("bank", b)], writes=[("apt", psl)])
                        for (_, off) in grp:
                            loc.append((psl, off))
                        flush_pending()
                        while pi_ < len(pvs) and pvs[pi_][0] < len(loc):
                            pending.append((pvs[pi_][1], loc, gidx_[0]))
                            pi_ += 1
                    assert pi_ == len(pvs)

                def pv(contribs, ob, col, rd):
                    def f(e):
                        r = None
                        for ci_, (l_ap, psl, pc) in enumerate(contribs):
                            r = e.matmul(bank[ob][0:65, col:col + 128], lhsT=l_ap, rhs=pts[psl][:, pc:pc + 128],
                                         start=(ci_ == 0), stop=(ci_ == len(contribs) - 1))
                        return r
                    P.op("pe", f, reads=rd + [("apt", psl) for (_, psl, _) in contribs], writes=[("bank", ob)])

                def banded(jobs, vtiles, vkeys, ntile, finish):
                    cur = {}
                    pvs = []
                    for I in range(ntile):
                        def cb(loc, I=I):
                            if I % 4 == 0:
                                cur["ob"] = 4 + ob_i[0] % 2
                                ob_i[0] += 1
                            ob = cur["ob"]
                            c_lo = (loc[I][0], loc[I][1] + (0 if I == 0 else 128))
                            c_hi = (loc[I + 1][0], loc[I + 1][1])
                            pv([(vtiles(I), c_lo[0], c_lo[1]), (vtiles(I + 1), c_hi[0], c_hi[1])], ob, (I % 4) * 128, vkeys)
                            if I % 4 == 3:
                                finish(I // 4, ob)
                        pvs.append((I + 1, cb))
                    run_jobs(jobs, pvs)

                jobs = []
                for J in range(17):
                    k_ap = kT[:, 128 * J:128 * J + 128]
                    if J == 0:
                        q_ap, b_ap, n_ = qT[:, 0:128], ab[:, h, 128:256], 128
                    elif J == 16:
                        q_ap, b_ap, n_ = qT[:, 1920:2048], ab[:, h, 0:128], 128
                    else:
                        q_ap, b_ap, n_ = qT[:, 128 * (J - 1):128 * (J + 1)], ab[:, h, 0:256], 256
                    jobs.append((k_ap, q_ap, b_ap, n_, kq + [kk, ("akpad", i)]))
                banded(jobs, lambda J: va[0][:, J, hh, :], [("va", 0), ("vaones", 0)], 16,
                       lambda g, ob: self.evac(acc[0:65, g * 512:(g + 1) * 512], bank[ob][0:65, :], reads=[("bank", ob)], writes=[("acc",)], eng="dve"))
                accv4 = acc.rearrange("p (i r) -> p r i", r=4)
                for r in range(4):
                    jobs = []
                    for J in range(5):
                        k_ap = k4[:, r, 128 * J:128 * J + 128]
                        if J == 0:
                            q_ap, b_ap, n_ = q4[hh][:, r, 0:128], ab[:, h, 256 + 128:256 + 256], 128
                        elif J == 4:
                            q_ap, b_ap, n_ = q4[hh][:, r, 384:512], ab[:, h, 256:256 + 128], 128
                        else:
                            q_ap, b_ap, n_ = q4[hh][:, r, 128 * (J - 1):128 * (J + 1)], ab[:, h, 256:512], 256
                        jobs.append((k_ap, q_ap, b_ap, n_, [("q4", hh), ("q4z", hh), ("k4",), ("k4pad",)]))
                    banded(jobs, lambda J, r=r: va[1][:, 5 * r + J, hh, :], [("va", 1), ("vaones", 1)], 4,
                           lambda g, ob, r=r: P.op("dve", lambda e: e.tensor_tensor(out=accv4[0:65, r, :], in0=bank[ob][0:65, :], in1=accv4[0:65, r, :], op=ALU.add),
                                                   reads=[("bank", ob), ("acc",)], writes=[("acc",)]))
                accv16 = acc.rearrange("p (m r) -> p m r", r=16)
                for r0 in range(0, 16, 4):
                    jobs = []
                    for r in range(r0, r0 + 4):
                        jobs.append((k16[:, r, :], q16[hh][:, r, :], ab[:, h, 512:640], 128, [("q16", hh), ("q16z", hh), ("k16",)]))
                    ob = 4 + ob_i[0] % 2
                    ob_i[0] += 1
                    pvs = []
                    for j in range(4):
                        def cb(loc, j=j, ob=ob, r0=r0):
                            pv([(va[2][:, r0 + j, hh, :], loc[j][0], loc[j][1])], ob, j * 128, [("va", 2), ("vaones", 2)])
                            if j == 3:
                                P.op("dve", lambda e: e.tensor_tensor(
                                    out=accv16[0:65, :, r0:r0 + 4], in0=bank[ob][0:65, :].rearrange("p (r m) -> p m r", r=4),
                                    in1=accv16[0:65, :, r0:r0 + 4], op=ALU.add),
                                    reads=[("bank", ob), ("acc",)], writes=[("acc",)])
                        pvs.append((j, cb))
                    run_jobs(jobs, pvs)
                flush_pending(True)
                if "acc" in self.debug.get("dump", ()) and h == self.debug.get("acc_head", 0):
                    self.dbg("acc", acc[0:65, :], [("acc",)])
                for (t0_, nt) in ((0, 7), (7, 7), (14, 2)):
                    b = 6 + tb_i[0] % 2
                    tb_i[0] += 1

                    def trf(e, t0_=t0_, nt=nt, b=b):
                        r = None
                        for j in range(nt):
                            t = t0_ + j
                            r = e.transpose(bank[b][:, j * 65:(j + 1) * 65], acc[0:65, t * 128:(t + 1) * 128], self.ident_f[0:65, 0:65])
                        return r
                    P.op("pe", trf, reads=[("acc",), ("ident_f",)], writes=[("bank", b)])
                    bv = bank[b][:, 0:nt * 65].rearrange("p (t c) -> p t c", t=nt)
                    P.op("dve", lambda e, bv=bv, t0_=t0_, nt=nt: e.reciprocal(out=rden[:, t0_:t0_ + nt], in_=bv[:, :, 64]),
                         reads=[("bank", b)], writes=[("rden",)])
                    P.op("dve", lambda e, bv=bv, t0_=t0_, nt=nt, h=h: e.tensor_tensor(
                        out=ytok[:, t0_:t0_ + nt, h * 64:(h + 1) * 64], in0=bv[:, :, 0:64],
                        in1=rden[:, t0_:t0_ + nt].unsqueeze(2).broadcast_to([128, nt, 64]), op=ALU.mult),
                        reads=[("bank", b), ("rden",)], writes=[("ytok", t) for t in range(t0_, t0_ + nt)])
        self.dbg("ytok", ytok.rearrange("p t c -> p (t c)"), [("ytok", t) for t in range(16)])
        reg.reset(m_fin)
        junk = reg.take(512, BF16, [("ajunk",)])
        for t in range(16):
            P.op("act", lambda e, t=t: e.activation(out=junk, in_=ytok[:, t, :], func=AF.Square, accum_out=ss[:, t:t + 1]),
                 reads=[("ytok", t)], writes=[("ajunk",), ("ass",)])
        P.op("act", lambda e: e.activation(out=ss, in_=ss, func=AF.Sqrt, bias=self.eps_ap, scale=1.0 / 512), reads=[("ass",), ("eps",)], writes=[("ass",)])
        P.op("dve", lambda e: e.reciprocal(out=ss, in_=ss), reads=[("ass",)], writes=[("ass",)])
        for t in range(16):
            P.op("dve", lambda e, t=t: e.tensor_scalar(out=ytok[:, t, :], in0=ytok[:, t, :], scalar1=ss[:, t:t + 1], scalar2=None, op0=ALU.mult),
                 reads=[("ytok", t), ("ass",)], writes=[("ytok", t)])
        self.to_feature_major(ytok, lambda t: ("ytok", t), 4, lambda cc, st_, sk: P.dma("sp", d["ynat"][cc], st_, reads=[sk], writes=[("ynat", cc)], slot="yn" + sk[0]))

    def to_feature_major(self, tok, tokkey, goff, sink=None, dst=None, dstkey=None):
        P = self.P
        bank = self.bank
        reg = self.reg
        if dst is None:
            stk = [(self.u("fmst"), 0) for _ in range(2)]
            st = [reg.take(S, BF16, [stk[i]]) for i in range(2)]
        for cc in range(4):
            if dst is None:
                s_ = st[cc % 2]
                sk = stk[cc % 2]
            else:
                s_ = dst[:, cc, :]
                sk = (dstkey, cc)
            for g in range(4):
                b = 6 + g % 2
                pb_ = bank[b].bitcast(BF16)

                def tr(e, cc=cc, g=g, pb_=pb_):
                    r = None
                    for j in range(4):
                        t = 4 * g + j
                        r = e.transpose(pb_[:, j * 128:(j + 1) * 128], tok[:, t, cc * 128:(cc + 1) * 128], self.ident_bf)
                    return r
                P.op("pe", tr, reads=[tokkey(t) for t in range(4 * g, 4 * g + 4)] + [("ident_bf",)], writes=[("bank", b)])
                P.op("dve", lambda e, s_=s_, g=g, pb_=pb_, cc=cc: e.tensor_scalar(
                    out=s_[:, g * 512:(g + 1) * 512], in0=pb_[:, 0:512], scalar1=self.gout[:, goff + cc:goff + cc + 1], scalar2=None, op0=ALU.mult),
                    reads=[("bank", b), ("gout",)], writes=[sk])
            if sink is not None:
                sink(cc, s_, sk)

    def hyena(self):
        P, d = self.P, self.dram
        reg = self.reg
        bank = self.bank
        U = [reg.take(16 * 512, BF16, [("hu", g, t) for t in range(16)]).rearrange("p (t c) -> p t c", t=16) for g in range(3)]
        m1 = reg.mark()
        diag = reg.take(36 * 128, BF16, [("diag",)]).rearrange("p (c i j) -> p c i j", c=12, i=3)
        cw = reg.take(36, F32, [("convw",)])
        cbf = reg.take(1536, BF16, [("convb",)])
        one0 = reg.take(128, BF16, [("one0",)])
        P.op("pool", lambda e: e.memset(cbf, 0.0), writes=[("convb",)])
        P.op("pool", lambda e: e.memset(one0, 0.0), writes=[("one0",)])
        P.op("pool", lambda e: e.memset(one0[0:1, :], 1.0), writes=[("one0",)])
        PT = [reg.take(4 * 2064, BF16, [("hpt", i), ("hptpad", i)]).rearrange("p (c t) -> p c t", c=4) for i in range(2)]
        P.dma("sp", cw, d["convw"], writes=[("convw",)])
        P.dma("pool", cbf[0:1, :], d["convb"], writes=[("convb",)])
        for ci in range(36):
            P.op("dve", lambda e, ci=ci: e.tensor_scalar(out=diag[:, ci // 3, ci % 3, :], in0=self.ident_f, scalar1=cw[:, ci:ci + 1], scalar2=None, op0=ALU.mult),
                 reads=[("convw",), ("ident_f",)], writes=[("diag",)])
        for i in range(2):
            P.op("pool", lambda e, i=i: e.memset(PT[i][:, :, 0:1], 0.0), writes=[("hptpad", i)])
            P.op("pool", lambda e, i=i: e.memset(PT[i][:, :, S + 1:S + 2], 0.0), writes=[("hptpad", i)])

        def load_grp(g):
            i = g % 2
            for c in range(4):
                P.dma("sp", PT[i][:, c, 1:S + 1], d["pt"][4 * g + c], reads=[("pt", 4 * g + c)], writes=[("hpt", i)], slot=f"hpt{i}_{c}")
        load_grp(0)
        bi = 0
        for g in range(3):
            if g + 1 < 3:
                load_grp(g + 1)
            i = g % 2
            for tt in range(16):
                b = bi % 4
                bi += 1

                def mmc(e, g=g, tt=tt, b=b, i=i):
                    r = None
                    for c in range(4):
                        o = bank[b][:, c * 128:(c + 1) * 128]
                        e.matmul(o, lhsT=one0, rhs=cbf[:, (4 * g + c) * 128:(4 * g + c + 1) * 128], start=True, stop=False)
                        for tap in range(3):
                            r = e.matmul(o, lhsT=PT[i][:, c, tt * 128 + tap:tt * 128 + tap + 128], rhs=diag[:, 4 * g + c, tap, :],
                                         start=False, stop=(tap == 2))
                    return r
                P.op("pe", mmc, reads=[("hpt", i), ("hptpad", i), ("diag",), ("convb",), ("one0",)], writes=[("bank", b)])
                self.evac(U[g][:, tt, :], bank[b][:], reads=[("bank", b)], writes=[("hu", g, tt)])
        for g in range(3):
            self.dbg(f"hu{g}", U[g].rearrange("p t c -> p (t c)"), [("hu", g, t) for t in range(16)])
        reg.reset(m1)
        Y = reg.take(32 * 512, BF16, [("hy", j) for j in range(32)]).rearrange("p (j c) -> p j c", j=32)
        cb = [reg.take(4096, BF16, [("hcb", i)]).rearrange("p (a j c) -> p a j c", a=2, j=16) for i in range(2)]
        hsl = [reg.take(1024, F32, [("hh", i)]).rearrange("p (a c) -> p a c", a=2) for i in range(2)]
        mt = [reg.take(512, F32, [("hm", i)]) for i in range(4)]
        cbi = [0]
        fb = [0]

        def load_cb(kt):
            si = cbi[0] % 2
            cbi[0] += 1
            P.dma("sp", cb[si].rearrange("p a j c -> p (a j c)"), d["cs_cb"][kt], writes=[("hcb", si)], slot=f"hcb{si}")
            return si

        def conv(o, IN, ink, G, gk, OUT, outk):
            si_next = load_cb(0)
            for kt in range(16):
                si = si_next
                hs_i = kt % 2
                P.dma("sp", hsl[hs_i].rearrange("p a c -> p (a c)"), d["hspec"][o, kt], reads=[("hspec", o, kt)], writes=[("hh", hs_i)], slot=f"hh{hs_i}")
                if kt + 1 < 16:
                    si_next = load_cb(kt + 1)
                ba = 2 * (fb[0] % 2)
                fb[0] += 1
                for part in range(2):
                    def mmf(e, part=part, si=si, ba=ba):
                        r = None
                        for jt in range(16):
                            r = e.matmul(bank[ba + part][:], lhsT=cb[si][:, part, jt, :], rhs=IN[:, jt, :], start=(jt == 0), stop=(jt == 15))
                        return r
                    P.op("pe", mmf, reads=[("hcb", si)] + [(ink[0], ink[1], t) for t in range(16)], writes=[("bank", ba + part)])
                uc, us = bank[ba], bank[ba + 1]
                hre, him = hsl[hs_i][:, 0, :], hsl[hs_i][:, 1, :]
                hk = ("hh", hs_i)
                P.op("dve", lambda e, uc=uc, hre=hre: e.tensor_tensor(out=mt[0], in0=uc[:], in1=hre, op=ALU.mult), reads=[("bank", ba), hk], writes=[("hm", 0)])
                P.op("dve", lambda e, us=us, him=him: e.tensor_tensor(out=mt[1], in0=us[:], in1=him, op=ALU.mult), reads=[("bank", ba + 1), hk], writes=[("hm", 1)])
                P.op("dve", lambda e, us=us, hre=hre: e.tensor_tensor(out=mt[2], in0=us[:], in1=hre, op=ALU.mult), reads=[("bank", ba + 1), hk], writes=[("hm", 2)])
                P.op("dve", lambda e, uc=uc, him=him: e.tensor_tensor(out=mt[3], in0=uc[:], in1=him, op=ALU.mult), reads=[("bank", ba), hk], writes=[("hm", 3)])
                P.op("pool", lambda e, kt=kt: e.tensor_tensor(out=Y[:, 2 * kt, :], in0=mt[0], in1=mt[1], op=ALU.add),
                     reads=[("hm", 0), ("hm", 1)], writes=[("hy", 2 * kt)])
                P.op("pool", lambda e, kt=kt: e.tensor_tensor(out=Y[:, 2 * kt + 1, :], in0=mt[2], in1=mt[3], op=ALU.subtract),
                     reads=[("hm", 2), ("hm", 3)], writes=[("hy", 2 * kt + 1)])
            si_next = load_cb(0)
            for nt in range(16):
                si = si_next
                if nt + 1 < 16:
                    si_next = load_cb(nt + 1)
                b = 4 + nt % 2

                def mmi(e, si=si, b=b):
                    r = None
                    for j in range(32):
                        r = e.matmul(bank[b][:], lhsT=cb[si][:, j % 2, j // 2, :], rhs=Y[:, j, :], start=(j == 0), stop=(j == 31))
                    return r
                P.op("pe", mmi, reads=[("hcb", si)] + [("hy", j) for j in range(32)], writes=[("bank", b)])
                P.op("dve", lambda e, b=b, nt=nt: e.tensor_tensor(out=OUT[:, nt, :], in0=bank[b][:], in1=G[:, nt, :], op=ALU.mult),
                     reads=[("bank", b), (gk[0], gk[1], nt)], writes=[(outk[0], outk[1], nt)])
        conv(0, U[0], ("hu", 0), U[1], ("hu", 1), U[0], ("hu", 0))
        self.dbg("z1", U[0].rearrange("p t c -> p (t c)"), [("hu", 0, t) for t in range(16)])
        conv(1, U[0], ("hu", 0), U[2], ("hu", 2), U[1], ("hu", 1))
        yh = U[1]
        self.dbg("yhy", yh.rearrange("p t c -> p (t c)"), [("hu", 1, t) for t in range(16)])
        reg.reset(m1)
        ss = reg.take(16, F32, [("hss",)])
        junk = reg.take(512, BF16, [("hjunk",)])
        ynh = reg.take(4 * S, BF16, [("ynh", c) for c in range(4)]).rearrange("p (c t) -> p c t", c=4)
        yna = reg.take(4 * S, BF16, [("yna", c) for c in range(4)]).rearrange("p (c t) -> p c t", c=4)
        for c in range(4):
            P.dma("sp", yna[:, c, :], d["ynat"][c], reads=[("ynat", c)], writes=[("yna", c)], slot=f"yna{c}")
        for t in range(16):
            P.op("act", lambda e, t=t: e.activation(out=junk, in_=yh[:, t, :], func=AF.Square, accum_out=ss[:, t:t + 1]),
                 reads=[("hu", 1, t)], writes=[("hjunk",), ("hss",)])
        P.op("act", lambda e: e.activation(out=ss, in_=ss, func=AF.Sqrt, bias=self.eps_ap, scale=1.0 / 512), reads=[("hss",), ("eps",)], writes=[("hss",)])
        P.op("dve", lambda e: e.reciprocal(out=ss, in_=ss), reads=[("hss",)], writes=[("hss",)])
        for t in range(16):
            P.op("dve", lambda e, t=t: e.tensor_scalar(out=yh[:, t, :], in0=yh[:, t, :], scalar1=ss[:, t:t + 1], scalar2=None, op0=ALU.mult),
                 reads=[("hu", 1, t), ("hss",)], writes=[("hu", 1, t)])
        self.to_feature_major(yh, lambda t: ("hu", 1, t), 0, dst=ynh, dstkey="ynh")
        X = self.X
        pre = not self.debug.get("skip_ffn2")
        if pre:
            sqb = self.alloc_sq("b")
            self.hT2 = self.new_hT("b")
        jobs = [(n, dc) for n in range(4) for dc in range(NK)]
        slots = {0: self.issue_wd(d["w_out_t"][jobs[0][1]], 1024)}
        bi = 0
        for ji, (n, dc) in enumerate(jobs):
            if ji + 1 < len(jobs):
                slots[ji + 1] = self.issue_wd(d["w_out_t"][jobs[ji + 1][1]], 1024)
            si = slots[ji]
            w = self.wd_slot[si]
            cs = slice(n * 512, (n + 1) * 512)
            b = bi % 4
            bi += 1

            def mmo(e, w=w, cs=cs, b=b):
                r = None
                for cc in range(8):
                    src_ = ynh[:, cc, cs] if cc < 4 else yna[:, cc - 4, cs]
                    r = e.matmul(bank[b][:], lhsT=w[:, cc, :], rhs=src_, start=(cc == 0), stop=(cc == 7))
                return r
            P.op("pe", mmo, reads=[("wd", si, 0), ("wd", si, 1)] + [("ynh", c) for c in range(4)] + [("yna", c) for c in range(4)],
                 writes=[("bank", b)])
            P.op("dve", lambda e, b=b, dc=dc, cs=cs: e.tensor_tensor(out=X[:, dc, cs], in0=bank[b][:], in1=X[:, dc, cs], op=ALU.add),
                 reads=[("bank", b), ("X", dc, n)], writes=[("X", dc, n)])
            if pre and dc == NK - 1:
                self.norm_to_hT(2, self.hT2, "b", sqb, (n,))

    def build(self):
        P, d = self.P, self.dram
        dbg = self.debug
        self.load_consts()
        if not dbg.get("skip_filter"):
            self.filter_phase()
            if "hspec" in dbg.get("dump", ()):
                o = self.nc.dram_tensor("dbg_hspec", [2, 16, 128, 1024], F32, kind="ExternalOutput").ap()
                self.final_toks.append(P.dma("sp", o, d["hspec"], reads=[("hspec", o_, kt) for o_ in range(2) for kt in range(16)],
                                             slot="dbg_hspec"))
        m0 = self.reg.mark()
        hT = self.new_hT("a")
        sqa = self.alloc_sq("a")
        self.mx_chunks_left = (0, 1, 2, 3)
        if not dbg.get("skip_ffn1"):
            self.norm_to_hT(0, hT, "a", sqa)
            self.reg.reset(m0)
            st = {}

            def hook1():
                st["sq"] = self.alloc_sq("mxa")
                self.norm_to_hT(1, hT, "a", st["sq"], (0, 1))
                self.mx_chunks_left = (2, 3)
            self.ffn(1, hT, "a", mid_hook=hook1)
            sqm = st["sq"]
        else:
            sqm = sqa
        if not dbg.get("skip_mixer"):
            self.mixer(hT, "a", sqm)
        self.reg.reset(m0)
        if not dbg.get("skip_ffn2"):
            if getattr(self, "hT2", None) is None:
                self.hT2 = self.new_hT("b")
                self.norm_to_hT(2, self.hT2, "b", self.alloc_sq("b0"))
                self.reg.reset(m0)
            self.ffn(2, self.hT2, "b", mid_hook=lambda: self.final(3, (0, 1)))
            self.final(3, (2, 3))
        else:
            self.final(3, (0, 1, 2, 3))
        P.emit(self.final_toks)
        P.close()
        return self.nc


def bf16(a):
    return np.asarray(a, np.float32).astype(ml_dtypes.bfloat16)


def make_consts():
    c = {}
    N = 2 * S
    n = np.arange(S, dtype=np.float64)
    th = 2.0 * math.pi * np.outer(n + 0.5, n + 0.5) / N
    Cm, Sm = np.cos(th), np.sin(th)
    cs = np.stack([Cm, Sm], 0).reshape(2, 16, 128, 16, 128)
    c["cs_cb"] = bf16(np.ascontiguousarray(cs.transpose(3, 2, 0, 1, 4)).reshape(16, 128, 4096))
    t = np.linspace(0.0, 1.0, S, dtype=np.float32)[:, None]
    w = (2.0 * math.pi * np.arange(S, dtype=np.float32)[:, None] / S).astype(np.float32)
    f = np.linspace(1e-4, 15, 16, dtype=np.float32)[None, :]
    z = np.concatenate([t, np.cos(f * w), -np.sin(f * w)], -1).astype(np.float32)
    c["zT"] = np.ascontiguousarray(z.T)
    deltas = np.linspace(math.log(1e-2) / 1.5, math.log(1e-2) / 0.3, 512, dtype=np.float32)
    decay = np.exp(-t * np.abs(deltas)[None, :]).astype(np.float32)
    dsh = np.concatenate([decay[1:], np.zeros((1, 512), np.float32)], 0)
    c["decay"] = np.ascontiguousarray(np.concatenate([decay.reshape(16, 128, 512), dsh.reshape(16, 128, 512)], -1))
    k = np.arange(S, dtype=np.float64)
    phi = math.pi * (k + 0.5) / N
    cph = (2.0 / N) * np.cos(phi)
    sph = (2.0 / N) * np.sin(phi)
    rot = np.concatenate([cph.reshape(16, 128).T, sph.reshape(16, 128).T, -sph.reshape(16, 128).T], 1)
    c["rot"] = np.ascontiguousarray(rot).astype(np.float32)
    c["ident"] = np.eye(128, dtype=np.float32)
    p = np.arange(128)[:, None]
    tabs = []
    for h in range(8):
        slope = 2.0 ** (-(h + 1))
        cc = np.arange(256)[None, :]
        dl = 64 + p - cc
        t1 = np.where(np.abs(dl) <= 64, -8.0 * slope * np.abs(dl), -240000.0)
        t2 = np.where(np.abs(dl) <= 64, -8.0 * slope * 4 * np.abs(dl), -240000.0)
        c3 = np.arange(128)[None, :]
        d3 = p - c3
        t3 = np.where(np.abs(d3) <= 64, -8.0 * slope * 16 * np.abs(d3), -240000.0)
        tabs.append(np.concatenate([t1, t2, t3], 1))
    c["abias"] = bf16(np.stack(tabs, 1).reshape(128, 8 * 640))
    return c


def prep_shared(inp):
    sh = dict(make_consts())
    f32 = lambda a: np.ascontiguousarray(np.asarray(a, np.float32))
    g = np.stack([inp["ffn1_norm_g"], inp["mix_norm_g"], inp["ffn2_norm_g"], inp["final_norm_g"]], 0)
    sh["gains"] = f32(g.reshape(4, NK, 128).transpose(2, 0, 1).reshape(128, 4 * NK))
    for i in (1, 2):
        wg = np.asarray(inp[f"ffn{i}_w_gate"], np.float32).reshape(NK, 128, NF, 128)
        wu = np.asarray(inp[f"ffn{i}_w_up"], np.float32).reshape(NK, 128, NF, 128)
        gu = np.stack([wg, wu], 0)
        sh[f"wgu{i}"] = f32(gu.transpose(3, 2, 0, 1, 4).reshape(NF, 128, 2048))
        wd = np.asarray(inp[f"ffn{i}_w_down"], np.float32).reshape(NF, 128, NK, 128)
        sh[f"wd{i}"] = f32(wd.transpose(2, 1, 0, 3).reshape(NK, 128, NF * 128))
    wi = np.asarray(inp["w_in"], np.float32).reshape(NK, 128, 12, 2, 128)
    sh["w_in_t"] = f32(wi.transpose(2, 1, 3, 0, 4).reshape(12, 128, 2048))
    wo = np.asarray(inp["w_out"], np.float32).reshape(NK, 128, NK, 128)
    sh["w_out_t"] = f32(wo.transpose(2, 1, 0, 3).reshape(NK, 128, 1024))
    go = np.concatenate([np.asarray(inp["hy_out_norm_g"]).reshape(4, 128), np.asarray(inp["attn_out_norm_g"]).reshape(4, 128)], 0)
    sh["gout"] = f32(go.T)
    cw = np.asarray(inp["hy_conv_w"], np.float32).reshape(3, 12, 128)
    sh["convw"] = f32(cw.transpose(2, 1, 0).reshape(128, 36))
    sh["convb"] = f32(np.asarray(inp["hy_conv_b"]).reshape(1, 1536))
    sh["filt_small"] = f32(np.stack([inp["hy_filt_b1"], inp["hy_filt_b2"], inp["hy_filt_b3"], inp["hy_filt_freq"]] +
                                    [np.zeros(64, np.float32)] * 4, 1))
    sh["fw1"] = f32(inp["hy_filt_w1"])
    sh["fw23"] = f32(np.concatenate([inp["hy_filt_w2"], inp["hy_filt_w3"]], 1))
    sh["fwout"] = f32(inp["hy_filt_w_out"])
    sh["skip"] = f32(inp["hy_filt_skip"])
    return sh


_CACHE = {}


def kernel(**inputs):
    inp = {k: np.asarray(v) for k, v in inputs.items()}
    x = inp["x"].astype(np.float32)
    sh = prep_shared(inp)
    if "nc" not in _CACHE:
        _CACHE["nc"] = Builder().build()
    nc = _CACHE["nc"]
    in_maps = []
    for b in range(NCORES):
        m = dict(sh)
        m["xT"] = np.ascontiguousarray(x[b].T)
        in_maps.append(m)
    res = run_bass_kernel_spmd(nc, in_maps, core_ids=list(range(NCORES)))
    out = np.stack([np.ascontiguousarray(res.results[b]["outT"].T) for b in range(NCORES)], 0)
    return out.astype(np.float32)
```
